# Optimizing a Trainium2 kernel written in Bass

```python
import math
import jax, jax.numpy as jnp
from jax import lax
import numpy as np

D_MODEL = 1024
BATCH = 8
SEQ = 4096
DEPTH = 1

HEAD_DIM = 64
N_HEADS = D_MODEL // HEAD_DIM
A_Q_HEADS = N_HEADS // 2
A_KV_HEADS = 2
A_GROUP = A_Q_HEADS // A_KV_HEADS
A_MAX_DIST = 127
B_HEADS = N_HEADS - A_Q_HEADS
B_PATTERNS = ((128, 1), (512, 4), (2048, 16))
BLOCK = 128
D_FF = 2816
CONV_WIDTH = 3
ALPHA = (2.0 * DEPTH) ** 0.25
BETA = (8.0 * DEPTH) ** -0.25
LN_EPS = 1e-5
RMS_EPS = 1e-6

A_Q_W = A_Q_HEADS * HEAD_DIM
A_KV_W = A_KV_HEADS * HEAD_DIM
B_W = B_HEADS * HEAD_DIM
IN_SPLITS = (A_Q_W, A_KV_W, A_KV_W, B_W, B_W, B_W)
IN_W = sum(IN_SPLITS)

kernel_name = "hymba_swa_sink_dilated_convffn_deepnorm"


def alibi_slopes(n):
    return jnp.asarray(np.array([2.0 ** (-8.0 * (i + 1) / n) for i in range(n)], dtype=np.float32))


def layer_norm(x, g, b):
    xf = x.astype(jnp.float32)
    mu = jnp.mean(xf, -1, keepdims=True)
    var = jnp.mean(jnp.square(xf - mu), -1, keepdims=True)
    return ((xf - mu) * lax.rsqrt(var + LN_EPS) * g.astype(jnp.float32) + b.astype(jnp.float32)).astype(x.dtype)


def rms_norm(x, g):
    xf = x.astype(jnp.float32)
    return (xf * lax.rsqrt(jnp.mean(jnp.square(xf), -1, keepdims=True) + RMS_EPS) * g.astype(jnp.float32)).astype(x.dtype)


def banded_attention(q, k, v, slope, max_dist, dist_unit, sink=None):
    L, dh = q.shape[-2], q.shape[-1]
    nb = -(-L // BLOCK)
    pad = nb * BLOCK - L
    q = jnp.pad(q, [(0, 0)] * (q.ndim - 2) + [(0, pad), (0, 0)])
    kv_pad = [(0, 0)] * (k.ndim - 2) + [(BLOCK, pad), (0, 0)]
    k = jnp.pad(k, kv_pad)
    v = jnp.pad(v, kv_pad)
    lead = k.shape[:-2]
    kb = k.reshape(*lead, nb + 1, BLOCK, dh)
    vb = v.reshape(*lead, nb + 1, BLOCK, dh)
    kw = jnp.concatenate([kb[..., :-1, :, :], kb[..., 1:, :, :]], axis=-2)
    vw = jnp.concatenate([vb[..., :-1, :, :], vb[..., 1:, :, :]], axis=-2)
    qb = q.reshape(*q.shape[:-2], nb, BLOCK, dh)
    s = jnp.einsum('...gnqd,...nkd->...gnqk', qb, kw,
                   preferred_element_type=jnp.float32) * (1.0 / math.sqrt(dh))
    qi = jnp.arange(BLOCK)[:, None]
    ki = jnp.arange(2 * BLOCK)[None, :]
    dist = BLOCK + qi - ki
    key_pos = jnp.arange(nb)[:, None, None] * BLOCK - BLOCK + ki[None]
    mask = (dist >= 0)[None] & (dist <= max_dist)[None] & (key_pos >= 0)
    s = s - slope.astype(jnp.float32) * (dist * dist_unit).astype(jnp.float32)
    s = jnp.where(mask, s, -jnp.inf)
    m = jnp.max(s, -1, keepdims=True)
    if sink is not None:
        sk = sink.astype(jnp.float32)
        m = jnp.maximum(m, sk)
        p = jnp.exp(s - m)
        l = jnp.sum(p, -1, keepdims=True) + jnp.exp(sk - m)
    else:
        p = jnp.exp(s - m)
        l = jnp.sum(p, -1, keepdims=True)
    o = jnp.einsum('...gnqk,...nkd->...gnqd', p, vw.astype(jnp.float32)) / l
    lse = (m + jnp.log(l))[..., 0]
    o = o.reshape(*o.shape[:-3], nb * BLOCK, dh)[..., :L, :].astype(q.dtype)
    lse = lse.reshape(*lse.shape[:-2], nb * BLOCK)[..., :L]
    return o, lse


def mixer_a(qa, ka, va, sinks):
    Bn, S, _ = qa.shape
    q = qa.reshape(Bn, S, A_KV_HEADS, A_GROUP, HEAD_DIM).transpose(0, 2, 3, 1, 4)
    k = ka.reshape(Bn, S, A_KV_HEADS, HEAD_DIM).transpose(0, 2, 1, 3)
    v = va.reshape(Bn, S, A_KV_HEADS, HEAD_DIM).transpose(0, 2, 1, 3)
    slope = alibi_slopes(A_Q_HEADS).reshape(A_KV_HEADS, A_GROUP, 1, 1, 1)
    sink = sinks.reshape(A_KV_HEADS, A_GROUP, 1, 1, 1)
    o, _ = banded_attention(q, k, v, slope, A_MAX_DIST, 1, sink)
    return o.transpose(0, 3, 1, 2, 4).reshape(Bn, S, A_Q_W)


def mixer_b(qb, kb, vb):
    Bn, S, _ = qb.shape
    q = qb.reshape(Bn, S, B_HEADS, HEAD_DIM).transpose(0, 2, 1, 3)
    k = kb.reshape(Bn, S, B_HEADS, HEAD_DIM).transpose(0, 2, 1, 3)
    v = vb.reshape(Bn, S, B_HEADS, HEAD_DIM).transpose(0, 2, 1, 3)
    slope = alibi_slopes(B_HEADS).reshape(B_HEADS, 1, 1, 1, 1, 1)
    outs, lses = [], []
    for (w, r) in B_PATTERNS:
        Lr = S // r
        qr = q.reshape(Bn, B_HEADS, Lr, r, HEAD_DIM).swapaxes(2, 3)[:, :, :, None]
        kr = k.reshape(Bn, B_HEADS, Lr, r, HEAD_DIM).swapaxes(2, 3)
        vr = v.reshape(Bn, B_HEADS, Lr, r, HEAD_DIM).swapaxes(2, 3)
        o, lse = banded_attention(qr, kr, vr, slope, w // r, r)
        outs.append(o[:, :, :, 0].swapaxes(2, 3).reshape(Bn, B_HEADS, S, HEAD_DIM))
        lses.append(lse[:, :, :, 0].swapaxes(2, 3).reshape(Bn, B_HEADS, S))
    wts = jax.nn.softmax(jnp.stack(lses, 0), axis=0)
    o = jnp.sum(wts[..., None] * jnp.stack(outs, 0).astype(jnp.float32), axis=0)
    return o.astype(qb.dtype).transpose(0, 2, 1, 3).reshape(Bn, S, B_W)


def causal_dwconv(u, w, b):
    K = w.shape[0]
    S = u.shape[1]
    up = jnp.pad(u, ((0, 0), (K - 1, 0), (0, 0)))
    y = up[:, 0:S, :] * w[0]
    for j in range(1, K):
        y = y + up[:, j:j + S, :] * w[j]
    return y + b


def setup_inputs(seed: int = 0) -> dict:
    key = jax.random.key(seed)
    ks = jax.random.split(key, 16)
    f32 = jnp.float32
    x = jax.random.normal(ks[0], (BATCH, SEQ, D_MODEL), f32)
    col_scale = jnp.concatenate([
        jnp.ones((A_Q_W + A_KV_W,), f32), jnp.full((A_KV_W,), BETA, f32),
        jnp.ones((2 * B_W,), f32), jnp.full((B_W,), BETA, f32)])
    w_in = jax.random.normal(ks[1], (D_MODEL, IN_W), f32) * D_MODEL ** -0.5 * col_scale
    norm_a_g = 1.0 + 0.02 * jax.random.normal(ks[2], (A_Q_W,), f32)
    norm_b_g = 1.0 + 0.02 * jax.random.normal(ks[3], (B_W,), f32)
    sinks_a = 0.5 * jax.random.normal(ks[4], (A_Q_HEADS,), f32)
    w_o = jax.random.normal(ks[5], (D_MODEL, D_MODEL), f32) * D_MODEL ** -0.5 * BETA
    ln1_g = 1.0 + 0.02 * jax.random.normal(ks[6], (D_MODEL,), f32)
    ln1_b = 0.02 * jax.random.normal(ks[7], (D_MODEL,), f32)
    w_up = jax.random.normal(ks[8], (D_MODEL, 2 * D_FF), f32) * D_MODEL ** -0.5 * BETA
    conv_w = jax.random.normal(ks[9], (CONV_WIDTH, 2 * D_FF), f32) * CONV_WIDTH ** -0.5
    conv_b = 0.02 * jax.random.normal(ks[10], (2 * D_FF,), f32)
    w_down = jax.random.normal(ks[11], (D_FF, D_MODEL), f32) * D_FF ** -0.5 * BETA
    ln2_g = 1.0 + 0.02 * jax.random.normal(ks[12], (D_MODEL,), f32)
    ln2_b = 0.02 * jax.random.normal(ks[13], (D_MODEL,), f32)
    return {"x": x, "w_in": w_in, "norm_a_g": norm_a_g, "norm_b_g": norm_b_g,
            "sinks_a": sinks_a, "w_o": w_o, "ln1_g": ln1_g, "ln1_b": ln1_b,
            "w_up": w_up, "conv_w": conv_w, "conv_b": conv_b, "w_down": w_down,
            "ln2_g": ln2_g, "ln2_b": ln2_b}


def reference(x, w_in, norm_a_g, norm_b_g, sinks_a, w_o, ln1_g, ln1_b,
              w_up, conv_w, conv_b, w_down, ln2_g, ln2_b):
    h = x
    for _ in range(DEPTH):
        proj = h @ w_in
        offs = np.cumsum((0,) + IN_SPLITS)
        qa, ka, va, qb, kb, vb = [proj[..., offs[i]:offs[i + 1]] for i in range(len(IN_SPLITS))]
        oa = rms_norm(mixer_a(qa, ka, va, sinks_a), norm_a_g)
        ob = rms_norm(mixer_b(qb, kb, vb), norm_b_g)
        mix = jnp.concatenate([oa, ob], axis=-1) @ w_o
        h = layer_norm(ALPHA * h + mix, ln1_g, ln1_b)
        u = causal_dwconv(h @ w_up, conv_w, conv_b)
        gate, val = u[..., :D_FF], u[..., D_FF:]
        ff = (jax.nn.gelu(gate) * val) @ w_down
        h = layer_norm(ALPHA * h + ff, ln2_g, ln2_b)
    return h
```

```python
import numpy as np
from contextlib import ExitStack
import concourse.bass as bass
import concourse.mybir as mybir
from concourse.bass_utils import run_bass_kernel_spmd

F32 = mybir.dt.float32
BF16 = mybir.dt.bfloat16
AF = mybir.ActivationFunctionType
ALU = mybir.AluOpType

S = 4096
D = 1024
DFF = 2816
NJ = DFF // 128
TT = 512
NTT = S // TT
ALPHA = 2.0 ** 0.25
LN_EPS = 1e-5
RMS_EPS = 1e-6
NCST = 224
C_GATT, C_L1G, C_L1B, C_L2G, C_L2B, C_CB, C_CW, C_SINK = 0, 8, 16, 24, 32, 40, 84, 216

ENGS = ("pe", "act", "dve", "pool", "sp")
PAIRS = list(range(8))
SKIP_NORM = False
DBG_LEVEL = 9


class Tok:
    __slots__ = ("writers", "readers", "war")

    def __init__(self):
        self.writers = []
        self.readers = []
        self.war = []


class Op:
    __slots__ = ("eng", "fn", "deps", "lane", "needed", "signal", "is_dma", "seq")

    def __init__(self, eng, fn, deps, lane, seq):
        self.eng = eng
        self.fn = fn
        self.deps = deps
        self.lane = lane
        self.is_dma = lane is not None
        self.needed = False
        self.signal = None
        self.seq = seq


class Prog:
    def __init__(self, nc):
        self.nc = nc
        self.ops = {e: [] for e in ENGS}
        self.seq = 0
        self.last_real = {}
        self.lane_last = {}

    def op(self, eng, fn, reads=(), writes=(), deps=(), lane=None):
        d = list(deps)
        for t in reads:
            d.extend(t.writers)
        for t in writes:
            d.extend(t.readers if t.readers else t.war)
        self.seq += 1
        o = Op(eng, fn, d, lane, self.seq)
        for t in reads:
            t.readers.append(o)
        for t in writes:
            if t.readers:
                t.war = t.readers
                t.readers = []
                t.writers = [o]
            else:
                t.writers.append(o)
        self.ops[eng].append(o)
        if fn is not None:
            if lane is not None:
                self.lane_last[lane] = o
            else:
                self.last_real[eng] = o
        return o

    def barrier(self):
        deps = list(self.last_real.values()) + list(self.lane_last.values())
        for e in ENGS:
            self.op(e, None, deps=deps)

    def emit(self):
        nc = self.nc
        for e in ENGS:
            for o in self.ops[e]:
                best = {}
                for d in o.deps:
                    if d is o:
                        continue
                    if (not d.is_dma) and d.eng == "pe" and o.eng == "pe" and not o.is_dma:
                        continue
                    key = ("L", d.lane) if d.is_dma else ("E", d.eng)
                    if key not in best or best[key].seq < d.seq:
                        best[key] = d
                o.deps = list(best.values())
                for d in o.deps:
                    d.needed = True
        with ExitStack() as es:
            sems = {e: es.enter_context(nc.semaphore("s_" + e)) for e in ENGS}
            counters = {e: 0 for e in ENGS}
            lanes = {}
            lanecnt = {}
            for e in ENGS:
                for o in self.ops[e]:
                    if o.fn is None:
                        continue
                    if o.is_dma:
                        if o.lane not in lanes:
                            lanes[o.lane] = es.enter_context(nc.semaphore("l_" + o.lane))
                            lanecnt[o.lane] = 0
                        lanecnt[o.lane] += 16
                        o.signal = (lanes[o.lane], lanecnt[o.lane])
                    elif o.needed:
                        counters[e] += 1
                        o.signal = (sems[e], counters[e])
            block = es.enter_context(nc.Block())

            def run(e, engine):
                waited = {}
                for o in self.ops[e]:
                    for d in o.deps:
                        s, v = d.signal
                        k = id(s)
                        if waited.get(k, 0) < v:
                            engine.wait_ge(s, v)
                            waited[k] = v
                    if o.fn is None:
                        continue
                    inst = o.fn(engine)
                    if o.is_dma:
                        inst.then_inc(o.signal[0], 16)
                    elif o.needed:
                        inst.then_inc(o.signal[0], 1)

            @block.tensor
            def _(eng):
                run("pe", eng)

            @block.scalar
            def _(eng):
                run("act", eng)

            @block.vector
            def _(eng):
                run("dve", eng)

            @block.gpsimd
            def _(eng):
                run("pool", eng)

            @block.sync
            def _(eng):
                run("sp", eng)


def build_nc(upto="C", debug=False):
    nc = bass.Bass("TRN2", target_bir_lowering=False)
    xT = nc.dram_tensor("xT", [D, S], F32, kind="ExternalInput").ap()
    w_in = nc.dram_tensor("w_in", [D, 2304], F32, kind="ExternalInput").ap()
    w_o = nc.dram_tensor("w_o", [D, D], F32, kind="ExternalInput").ap()
    w_up = nc.dram_tensor("w_up", [D, 2 * DFF], F32, kind="ExternalInput").ap()
    w_down = nc.dram_tensor("w_down", [DFF, D], F32, kind="ExternalInput").ap()
    cst_d = nc.dram_tensor("cst", [128, NCST], F32, kind="ExternalInput").ap()
    etab_d = nc.dram_tensor("etab", [128, 32, 512], F32, kind="ExternalInput").ap()
    yT = nc.dram_tensor("yT", [D, S], F32, kind="ExternalOutput").ap()
    ikind = "ExternalOutput" if debug else "Internal"
    projT_d = nc.dram_tensor("projT_d", [18, 128, S], BF16, kind=ikind).ap()
    accT_d = nc.dram_tensor("accT_d", [8, 128, S], BF16, kind=ikind).ap()
    wup_d = nc.dram_tensor("wup_d", [128, 8, 2 * DFF], BF16, kind="Internal").ap()
    wo_d = nc.dram_tensor("wo_d", [128, 8, D], BF16, kind="Internal").ap()
    wdn_d = nc.dram_tensor("wdn_d", [128, 8, NJ, 128], BF16, kind="Internal").ap()

    xT_v = xT.rearrange("(kc p) t -> p kc t", p=128)
    yT_v = yT.rearrange("(kc p) t -> p kc t", p=128)
    w_in_v = w_in.rearrange("(kc p) c -> p kc c", p=128)
    w_o_v = w_o.rearrange("(kc p) c -> p kc c", p=128)
    w_up_v = w_up.rearrange("(kc p) c -> p kc c", p=128)
    w_down_v = w_down.rearrange("(j p) c -> p j c", p=128)
    projT_v = projT_d.rearrange("c p t -> p c t")
    accT_v = accT_d.rearrange("c p t -> p c t")

    P = Prog(nc)

    def dma(eng, out, in_, lane, reads=(), writes=()):
        return P.op(eng, lambda e, o=out, i=in_: e.dma_start(out=o, in_=i), reads, writes, lane=lane)

    def mm(out, lhsT, rhs, start, stop, reads=(), writes=()):
        return P.op("pe", lambda e, o=out, l=lhsT, r=rhs, s=start, t=stop:
                    e.matmul(o, lhsT=l, rhs=r, start=s, stop=t), reads, writes)

    def act(out, in_, func, reads=(), writes=(), scale=None, bias=None):
        def f(e, o=out, i=in_, fn=func, sc=scale, bi=bias):
            kw = {}
            if sc is not None:
                kw["scale"] = sc
            if bi is not None:
                kw["bias"] = bi
            return e.activation(out=o, in_=i, func=fn, **kw)
        return P.op("act", f, reads, writes)

    def tt_(eng, out, in0, in1, op, reads=(), writes=()):
        return P.op(eng, lambda e, o=out, a=in0, b=in1, p=op: e.tensor_tensor(out=o, in0=a, in1=b, op=p),
                    reads, writes)

    def stt(eng, out, in0, scalar, in1, op0, op1, reads=(), writes=()):
        return P.op(eng, lambda e, o=out, a=in0, s=scalar, b=in1, p0=op0, p1=op1:
                    e.scalar_tensor_tensor(out=o, in0=a, scalar=s, in1=b, op0=p0, op1=p1), reads, writes)

    def ts(eng, out, in0, s1, s2, op0, op1, reads=(), writes=()):
        return P.op(eng, lambda e, o=out, a=in0, x=s1, y=s2, p0=op0, p1=op1:
                    e.tensor_scalar(out=o, in0=a, scalar1=x, scalar2=y, op0=p0, op1=p1), reads, writes)

    def cp(eng, out, in_, reads=(), writes=()):
        if eng == "act":
            return act(out, in_, AF.Copy, reads, writes)
        return P.op(eng, lambda e, o=out, i=in_: e.tensor_copy(out=o, in_=i), reads, writes)

    def mset(eng, ap, val, writes=()):
        return P.op(eng, lambda e, a=ap, v=val: e.memset(a, v), (), writes)

    with ExitStack() as es0:
        def sb(es, name, shape, dt):
            return es.enter_context(nc.sbuf_tensor("sb_" + name, shape, dt))

        pall = es0.enter_context(nc.psum_tensor("pall", [128, 8, 512], F32))
        banks = [pall[:, i, :] for i in range(8)]
        bt = pall[:, 7, :].bitcast(BF16)
        tbk = [Tok() for _ in range(8)]
        tbt = tbk[7]

        cst = sb(es0, "cst", [128, NCST], F32)
        es_sb = sb(es0, "es_sb", [128, 8], F32)
        identf = sb(es0, "identf", [128, 128], F32)
        ident = sb(es0, "ident", [128, 128], BF16)
        ones = sb(es0, "ones", [128, 128], BF16)
        epsl = sb(es0, "epsl", [128, 2], F32)
        t_cst, t_es, t_id, t_ones, t_eps = Tok(), Tok(), Tok(), Tok(), Tok()
        t_wup = [Tok() for _ in range(8)]
        t_wod, t_wdn = Tok(), Tok()

        dma("sp", cst[:], cst_d, "cst", writes=[t_cst])
        act(es_sb[:], cst[:, C_SINK:C_SINK + 8], AF.Exp, reads=[t_cst], writes=[t_es])
        mset("pool", identf[:], 0.0, writes=[t_id])
        P.op("pool", lambda e: e.affine_select(out=identf[:], in_=identf[:], pattern=[[-1, 128]],
                                               compare_op=ALU.not_equal, fill=1.0, base=0,
                                               channel_multiplier=1), reads=[t_id], writes=[t_id])
        cp("dve", ident[:], identf[:], reads=[t_id], writes=[t_id])
        mset("dve", ones[:], 1.0, writes=[t_ones])
        mset("dve", epsl[:, 0:1], LN_EPS, writes=[t_eps])
        mset("dve", epsl[:, 1:2], RMS_EPS, writes=[t_eps])

        t_proj = [Tok() for _ in range(18)]
        with ExitStack() as esA:
            w_in_bf = sb(esA, "w_in_bf", [128, 8, 2304], BF16)
            xb = [sb(esA, f"xb{i}", [128, 8, TT], BF16) for i in range(2)]
            pj = [sb(esA, f"pj{i}", [128, 18, TT], BF16) for i in range(2)]
            t_win = [Tok() for _ in range(8)]
            t_xb = [Tok(), Tok()]
            t_pj = [Tok(), Tok()]
            dma("pool", xb[0][:], xT_v[:, :, 0:TT], "xb0", writes=[t_xb[0]])
            for kc in range(8):
                dma("pool", w_in_bf[:, kc, :], w_in_v[:, kc, :], f"win{kc}", writes=[t_win[kc]])
            dma("pool", xb[1][:], xT_v[:, :, TT:2 * TT], "xb1", writes=[t_xb[1]])
            for tti in range(NTT):
                b = tti % 2
                if 1 <= tti and tti + 1 < NTT:
                    nb_ = (tti + 1) % 2
                    dma("pool", xb[nb_][:], xT_v[:, :, (tti + 1) * TT:(tti + 2) * TT], f"xb{nb_}", writes=[t_xb[nb_]])
                dma("pool", wup_d[:, tti, :], w_up_v[:, tti, :], "wupc", writes=[t_wup[tti]])
                dma("pool", wdn_d[:, tti, :, :], w_down_v[:, :, tti * 128:(tti + 1) * 128], "wdnc", writes=[t_wdn])
                if tti == 0:
                    dma("pool", wo_d, w_o_v, "woc", writes=[t_wod])
                for c0 in range(0, 18, 2):
                    for kc in range(8):
                        for c in (c0, c0 + 1):
                            bk = c % 8
                            mm(banks[bk][:, :], w_in_bf[:, kc, c * 128:(c + 1) * 128], xb[b][:, kc, :],
                               kc == 0, kc == 7, reads=[t_win[kc], t_xb[b]], writes=[tbk[bk]])
                    for c in (c0, c0 + 1):
                        bk = c % 8
                        cp("act" if c % 2 == 0 else "dve", pj[b][:, c, :], banks[bk][:, :],
                           reads=[tbk[bk]], writes=[t_pj[b]])
                dma("sp", projT_v[:, :, tti * TT:(tti + 1) * TT], pj[b][:], f"pj{b}",
                    reads=[t_pj[b]], writes=t_proj)
        P.barrier()

        t_acc_d = [Tok() for _ in range(8)]
        with ExitStack() as esB:
          if upto in ("B", "C"):
            epair = sb(esB, "epair", [128, 32, 512], BF16)
            qT = [sb(esB, f"qT{i}", [128, S], BF16) for i in range(2)]
            kT = [sb(esB, f"kT{i}", [128, S], BF16) for i in range(2)]
            vT = [sb(esB, f"vT{i}", [128, S], BF16) for i in range(2)]
            vaug = [sb(esB, f"vaug{i}", [128, 32, 2, 128], BF16) for i in range(2)]
            acc32 = sb(esB, "acc32", [128, 2, S], F32)
            NPT = 5
            pt = [sb(esB, f"pt{i}", [128, 2, 256], BF16) for i in range(NPT)]
            lnl = [sb(esB, f"lnl{i}", [128, 1024], F32) for i in range(2)]
            rec = [sb(esB, f"rec{i}", [128, 1024], F32) for i in range(2)]
            oT = [sb(esB, f"oT{i}", [128, S], BF16) for i in range(2)]
            t_ep = Tok()
            t_q = [Tok(), Tok()]
            t_k = [Tok(), Tok()]
            t_v = [Tok(), Tok()]
            t_va = [[Tok() for _ in range(4)] for _ in range(2)]
            t_acc = [Tok() for _ in range(32)]
            t_pt = [Tok() for _ in range(NPT)]
            t_lnl = [Tok(), Tok()]
            t_rec = [Tok(), Tok()]
            t_oT = [Tok(), Tok()]

            for h4 in range(8):
                dma("pool", epair[:, 4 * h4:4 * h4 + 4, :], etab_d[:, 4 * h4:4 * h4 + 4, :], "etab", writes=[t_ep])
            for i in range(2):
                mset("pool", vaug[i][:, :, :, 64:128], 1.0, writes=t_va[i])

            def pair_chunks(p):
                if p < 4:
                    return p, 4, 5
                j = p - 4
                return 6 + j, 10 + j, 14 + j

            def issue_loads(p):
                cq, ck, cv = pair_chunks(p)
                qi = p % 2
                dma("sp", qT[qi][:], projT_d[cq], f"q{qi}", reads=[t_proj[cq]], writes=[t_q[qi]])
                if p == PAIRS[0] or p >= 4:
                    ki = 0 if p < 4 else (p + 1) % 2
                    dma("sp", kT[ki][:], projT_d[ck], f"k{ki}", reads=[t_proj[ck]], writes=[t_k[ki]])
                    dma("sp", vT[ki][:], projT_d[cv], f"v{ki}", reads=[t_proj[cv]], writes=[t_v[ki]])

            issue_loads(PAIRS[0])
            npat = 0
            bseq = 0
            oseq = 0
            for pidx, p in enumerate(PAIRS):
                if pidx + 1 < len(PAIRS):
                    issue_loads(PAIRS[pidx + 1])
                qi = p % 2
                ki = 0 if p < 4 else (p + 1) % 2
                pats = [(1, 0)] if p < 4 else [(1, 0), (4, 1), (16, 2)]
                for (r, pi) in pats:
                    ti = p if p < 4 else 4 + (p - 4) * 3 + pi
                    nb = 32 // r
                    reuse_v = (0 < p < 4) and PAIRS[0] == 0
                    if reuse_v:
                        va, tva = vaug[0], t_va[0]
                    else:
                        va = vaug[npat % 2]
                        tva = t_va[npat % 2]
                        npat += 1
                    blocks = [(rho, n) for rho in range(r) for n in range(nb)]

                    def tstart(rho, n):
                        return rho + r * 128 * n

                    for g in range(4 if (DBG_LEVEL >= 1 and not reuse_v) else 0):
                        for i in range(8):
                            rho, n = blocks[8 * g + i]
                            st = tstart(rho, n)
                            P.op("pe", lambda e, o=bt[:, i * 128:(i + 1) * 128],
                                 a=vT[ki][:, st:st + r * 127 + 1:r]: e.transpose(o, a, ident[:]),
                                 reads=[t_v[ki], t_id], writes=[tbt])
                        cp("act", va[:, 8 * g:8 * g + 8, :, 0:64],
                           bt[:, :].rearrange("p (b h d) -> p b h d", b=8, h=2),
                           reads=[tbt], writes=[tva[g]])

                    info = {}

                    def emit_scores(bi):
                        nonlocal bseq
                        rho, n = blocks[bi]
                        st = tstart(rho, n)
                        last = (n == nb - 1)
                        nq = 128 if last else 256
                        sbk = 2 * (bseq % 2)
                        pti = bseq % NPT
                        bseq += 1
                        info[bi] = pti
                        use_bias = False
                        if use_bias:
                            for h in range(2):
                                mm(banks[sbk + h][:, 0:nq], ident[:], epair[:, 16 + ti, h * 256:h * 256 + nq],
                                   True, False, reads=[t_id, t_ep], writes=[tbk[sbk + h]])
                        for h in range(2):
                            mm(banks[sbk + h][:, 0:nq],
                               kT[ki][h * 64:(h + 1) * 64, st:st + r * 127 + 1:r],
                               qT[qi][h * 64:(h + 1) * 64, st:st + r * (nq - 1) + 1:r],
                               not use_bias, True, reads=[t_k[ki], t_q[qi]], writes=[tbk[sbk + h]])
                        sv = pall[:, sbk:sbk + 2, 0:nq]
                        act(pt[pti][:, :, 0:nq], sv, AF.Exp, reads=[tbk[sbk], tbk[sbk + 1]], writes=[t_pt[pti]],
                            scale=0.125)
                        if not use_bias:
                            ev = epair[:, ti, :].rearrange("p (h q) -> p h q", h=2)[:, :, 0:nq]
                            tt_("dve", pt[pti][:, :, 0:nq], pt[pti][:, :, 0:nq], ev, ALU.mult,
                                reads=[t_pt[pti], t_ep], writes=[t_pt[pti]])

                    def emit_pv(bi, first):
                        nonlocal oseq
                        rho, n = blocks[bi]
                        obk = 4 + ((oseq // 2) % 3)
                        oseq += 1
                        pc = info[bi]
                        blk = bi
                        for h in range(2):
                            c0 = h * 256 + (n % 2) * 128
                            o_ap = banks[obk][:, c0:c0 + 128]
                            if n > 0:
                                pp = info[bi - 1]
                                mm(o_ap, va[:, blk - 1, h, :], pt[pp][:, h, 128:256], True, False,
                                   reads=[tva[(blk - 1) // 8], t_pt[pp]], writes=[tbk[obk]])
                                mm(o_ap, va[:, blk, h, :], pt[pc][:, h, 0:128], False, True,
                                   reads=[tva[blk // 8], t_pt[pc]], writes=[tbk[obk]])
                            else:
                                mm(o_ap, va[:, blk, h, :], pt[pc][:, h, 0:128], True, True,
                                   reads=[tva[blk // 8], t_pt[pc]], writes=[tbk[obk]])
                        if n % 2 == 0:
                            return
                        st0 = tstart(rho, n - 1)
                        ov = banks[obk][:, :].rearrange("p (h q) -> p h q", h=2)
                        av = acc32[:, :, st0:st0 + r * 255 + 1:r]
                        nb0 = st0 // 128
                        nb1 = (st0 + r * 255) // 128
                        toks = t_acc[nb0:nb1 + 1]
                        if first:
                            cp("dve", av, ov, reads=[tbk[obk]], writes=toks)
                        else:
                            tt_("dve", av, ov, av, ALU.add, reads=[tbk[obk]] + toks, writes=toks)

                    LOOK = 2
                    for i in range(len(blocks) + LOOK):
                        if i < len(blocks) and DBG_LEVEL >= 2:
                            emit_scores(i)
                        if i - LOOK >= 0 and DBG_LEVEL >= 3:
                            emit_pv(i - LOOK, pi == 0)

                oi = p % 2
                for h in range(0 if not SKIP_NORM else 2, 2):
                    head = (p + 4 * h) if p < 4 else 0
                    for c4 in range(4):
                        sl = slice(c4 * 1024, (c4 + 1) * 1024)
                        toks = t_acc[c4 * 8:(c4 + 1) * 8]
                        li = (h * 4 + c4) % 2
                        bias = es_sb[64:128, head:head + 1] if p < 4 else None
                        act(lnl[li][0:64, :], acc32[64:128, h, sl], AF.Ln,
                            reads=toks + [t_es], writes=[t_lnl[li]], bias=bias)
                        act(rec[li][0:64, :], lnl[li][0:64, :], AF.Exp, reads=[t_lnl[li]],
                            writes=[t_rec[li]], scale=-1.0)
                        tt_("dve", oT[oi][h * 64:(h + 1) * 64, sl], acc32[0:64, h, sl], rec[li][0:64, :], ALU.mult,
                            reads=toks + [t_rec[li]], writes=[t_oT[oi]])
                dma("sp", accT_d[p], oT[oi][:], f"oT{oi}", reads=[t_oT[oi]], writes=[t_acc_d[p]])
        P.barrier()

        with ExitStack() as esC:
          if upto == "C":
            wo_bf = sb(esC, "wo_bf", [128, 8, D], BF16)
            NWD = 2
            wd = [sb(esC, f"wd{i}", [128, NJ, 128], BF16) for i in range(NWD)]
            wu = [sb(esC, f"wu{i}", [128, 8, 512], BF16) for i in range(2)]
            at = sb(esC, "at", [128, 8, TT], BF16)
            sq = sb(esC, "sq", [128, 8, TT], BF16)
            x32 = sb(esC, "x32", [128, 8, TT], F32)
            h1 = [sb(esC, f"h1_{i}", [128, 8, TT], F32) for i in range(3)]
            h1b = [sb(esC, f"h1b_{i}", [128, 8, TT], BF16) for i in range(2)]
            gTs = [sb(esC, f"gT{i}", [128, NJ, TT], BF16) for i in range(2)]
            ygs = [sb(esC, f"yg{i}", [128, TT], F32) for i in range(2)]
            yvs = [sb(esC, f"yv{i}", [128, TT], F32) for i in range(2)]
            tmpA = [sb(esC, f"tmpA{i}", [128, TT], F32) for i in range(2)]
            stA = [sb(esC, f"stA{i}", [128, TT], F32) for i in range(4)]
            uhs = [sb(esC, f"uh{i}", [128, 2 * NJ, 2], F32) for i in range(2)]
            t_wo = Tok()
            t_wd = [Tok() for _ in range(NWD)]
            t_wu = [Tok(), Tok()]
            t_at = [Tok() for _ in range(8)]
            t_sq = [Tok() for _ in range(8)]
            t_x32 = [Tok() for _ in range(8)]
            t_h1 = [[Tok() for _ in range(8)] for _ in range(3)]
            t_h1b = [[Tok() for _ in range(8)] for _ in range(2)]
            t_gTs = [[Tok() for _ in range(NJ)] for _ in range(2)]
            t_ygs, t_yvs = [Tok(), Tok()], [Tok(), Tok()]
            t_tmpA = [Tok(), Tok()]
            t_stA = [Tok() for _ in range(4)]
            t_uhs = [[Tok() for _ in range(2 * NJ)] for _ in range(2)]

            dma("sp", wo_bf[:], wo_d, "wo", reads=[t_wod], writes=[t_wo])

            def col(c):
                return cst[:, c:c + 1]

            def ln_units(src, t_src, gcol, bcol, dst32, t_d32, dstbf, t_dbf, cpb, t_cpb, sqb, t_sqb,
                         st, t_st, tmp, t_tmp, bkA, bkB):
                mean, mv, lr, nmr = st
                units = []

                def u_sq(m0):
                    def f():
                        for m in range(m0, m0 + 2):
                            cp("dve", cpb(m), src[:, m, :], reads=[t_src[m]], writes=[t_cpb[m]])
                            act(sqb(m), src[:, m, :], AF.Square, reads=[t_src[m]], writes=[t_sqb[m]])
                    return f
                units += [u_sq(0), u_sq(2), u_sq(4), u_sq(6)]

                def u_mm():
                    for m in range(8):
                        mm(banks[bkA][:, :], ones[:], cpb(m), m == 0, m == 7, reads=[t_ones, t_cpb[m]], writes=[tbk[bkA]])
                    for m in range(8):
                        mm(banks[bkB][:, :], ones[:], sqb(m), m == 0, m == 7, reads=[t_ones, t_sqb[m]], writes=[tbk[bkB]])
                units.append(u_mm)

                def u_stats():
                    act(mean[:], banks[bkA][:, :], AF.Copy, reads=[tbk[bkA]], writes=[t_st[0]], scale=1.0 / D)
                    act(mv[:], banks[bkA][:, :], AF.Square, reads=[tbk[bkA]], writes=[t_st[1]], scale=1.0 / D)
                    stt("dve", mv[:], banks[bkB][:, :], 1.0 / D, mv[:], ALU.mult, ALU.subtract,
                        reads=[tbk[bkB], t_st[1]], writes=[t_st[1]])
                    act(lr[:], mv[:], AF.Ln, reads=[t_st[1], t_eps], writes=[t_st[2]], bias=epsl[:, 0:1])
                    act(lr[:], lr[:], AF.Exp, reads=[t_st[2]], writes=[t_st[2]], scale=-0.5)
                    stt("dve", nmr[:], mean[:], -1.0, lr[:], ALU.mult, ALU.mult,
                        reads=[t_st[0], t_st[2]], writes=[t_st[3]])
                units.append(u_stats)

                def u_norm(m0):
                    def f():
                        for m in range(m0, m0 + 2):
                            tb = m % 2
                            tt_("dve", tmp[tb][:], src[:, m, :], lr[:], ALU.mult, reads=[t_src[m], t_st[2]], writes=[t_tmp[tb]])
                        for m in range(m0, m0 + 2):
                            tb = m % 2
                            tt_("dve", tmp[tb][:], tmp[tb][:], nmr[:], ALU.add, reads=[t_tmp[tb], t_st[3]], writes=[t_tmp[tb]])
                        for m in range(m0, m0 + 2):
                            tb = m % 2
                            ts("pool", dst32[:, m, :], tmp[tb][:], col(gcol + m), col(bcol + m), ALU.mult, ALU.add,
                               reads=[t_tmp[tb], t_cst], writes=[t_d32[m]])
                            if dstbf is not None:
                                ts("pool", dstbf[:, m, :], tmp[tb][:], col(gcol + m), col(bcol + m), ALU.mult, ALU.add,
                                   reads=[t_tmp[tb], t_cst], writes=[t_dbf[m]])
                    return f
                units += [u_norm(0), u_norm(2), u_norm(4), u_norm(6)]
                return units

            wu_list = [(t, jj) for t in range(NTT) for jj in range(NJ // 2)]
            wu_issued = [0]

            def wu_issue_upto(k):
                while wu_issued[0] <= k and wu_issued[0] < len(wu_list):
                    i = wu_issued[0]
                    b = i % 2
                    jj = wu_list[i][1]
                    dma("sp", wu[b][:], wup_d[:, :, jj * 512:(jj + 1) * 512], f"wu{b}", reads=t_wup, writes=[t_wu[b]])
                    wu_issued[0] += 1

            wd_issued = [0]

            def wd_issue_upto(k):
                while wd_issued[0] <= k and wd_issued[0] < NTT * 8:
                    i = wd_issued[0]
                    b = i % NWD
                    m = i % 8
                    dma("sp", wd[b][:], wdn_d[:, m, :, :], f"wdl{b}", reads=[t_wdn], writes=[t_wd[b]])
                    wd_issued[0] += 1

            def F_units(t):
                hb = t % 2
                h3 = t % 3
                tsl = slice(t * TT, (t + 1) * TT)
                units = []

                def u_load():
                    for m in range(8):
                        dma("sp", at[:, m, :], accT_v[:, m, tsl], f"at{m}", reads=[t_acc_d[m]], writes=[t_at[m]])
                    for m in range(8):
                        dma("sp", x32[:, m, :], xT_v[:, m, tsl], f"x32_{m}", writes=[t_x32[m]])
                units.append(u_load)

                def u_rsq(c0):
                    def f():
                        for c in range(c0, c0 + 4):
                            tt_("dve", sq[:, c, :], at[:, c, :], at[:, c, :], ALU.mult, reads=[t_at[c]], writes=[t_sq[c]])
                    return f
                units += [u_rsq(0), u_rsq(4)]

                def u_rmm():
                    for gi in range(2):
                        for i, c in enumerate(range(4 * gi, 4 * gi + 4)):
                            mm(banks[gi][:, :], ones[:], sq[:, c, :], i == 0, i == 3, reads=[t_ones, t_sq[c]], writes=[tbk[gi]])
                        a = stA[2 * gi]
                        act(a[:], banks[gi][:, :], AF.Ln, reads=[tbk[gi], t_eps], writes=[t_stA[2 * gi]],
                            scale=1.0 / 512, bias=epsl[:, 1:2])
                        act(a[:], a[:], AF.Exp, reads=[t_stA[2 * gi]], writes=[t_stA[2 * gi]], scale=-0.5)
                units.append(u_rmm)

                def u_rn():
                    for gi in range(2):
                        for c in range(4 * gi, 4 * gi + 4):
                            stt("dve", at[:, c, :], at[:, c, :], col(C_GATT + c), stA[2 * gi][:], ALU.mult, ALU.mult,
                                reads=[t_at[c], t_stA[2 * gi], t_cst], writes=[t_at[c]])
                units.append(u_rn)

                def u_wo(m):
                    def f():
                        bk = m % 2
                        for kc in range(8):
                            mm(banks[bk][:, :], wo_bf[:, kc, m * 128:(m + 1) * 128], at[:, kc, :], kc == 0, kc == 7,
                               reads=[t_wo, t_at[kc]], writes=[tbk[bk]])
                        stt("dve", x32[:, m, :], x32[:, m, :], ALPHA, banks[bk][:, :], ALU.mult, ALU.add,
                            reads=[t_x32[m], tbk[bk]], writes=[t_x32[m]])
                    return f
                units += [u_wo(m) for m in range(8)]
                units += ln_units(x32, t_x32, C_L1G, C_L1B, h1[h3], t_h1[h3], h1b[hb], t_h1b[hb],
                                  lambda m: at[:, m, :], t_at, lambda m: sq[:, m, :], t_sq,
                                  stA, t_stA, tmpA, t_tmpA, 0, 1)
                return units

            UPB = [2, 3, 4, 5, 7]

            def up_units(t):
                hb = t % 2
                gT, t_gT = gTs[t % 2], t_gTs[t % 2]
                units = []

                def u_up(j):
                    def f():
                        k = t * (NJ // 2) + j // 2
                        if j % 2 == 0:
                            wu_issue_upto(k + 1)
                        cur = k % 2
                        woff = (j % 2) * 256
                        yg, yv, t_yg, t_yv = ygs[j % 2], yvs[j % 2], t_ygs[j % 2], t_yvs[j % 2]
                        hv = []
                        for kc in range(8):
                            for half in range(2):
                                bk = UPB[(2 * j + half) % 5]
                                mm(banks[bk][:, :], wu[cur][:, kc, woff + half * 128:woff + half * 128 + 128],
                                   h1b[hb][:, kc, :], kc == 0, kc == 7, reads=[t_wu[cur], t_h1b[hb][kc]], writes=[tbk[bk]])
                        for half, (ybuf, t_y) in enumerate(((yg, t_yg), (yv, t_yv))):
                            bk = UPB[(2 * j + half) % 5]
                            cc = j + half * NJ
                            w0, w1, w2 = (col(C_CW + cc * 3 + kk) for kk in range(3))
                            hv.append((ybuf, t_y, bk, cc, w0, w1, w2))
                        for (ybuf, t_y, bk, cc, w0, w1, w2) in hv:
                            act(ybuf[:], banks[bk][:, :], AF.Identity, reads=[tbk[bk], t_cst], writes=[t_y],
                                scale=w2, bias=col(C_CB + cc))
                        for (ybuf, t_y, bk, cc, w0, w1, w2) in hv:
                            stt("dve", ybuf[:, 1:TT], banks[bk][:, 0:TT - 1], w1, ybuf[:, 1:TT], ALU.mult, ALU.add,
                                reads=[tbk[bk], t_y], writes=[t_y])
                        for (ybuf, t_y, bk, cc, w0, w1, w2) in hv:
                            stt("dve", ybuf[:, 2:TT], banks[bk][:, 0:TT - 2], w0, ybuf[:, 2:TT], ALU.mult, ALU.add,
                                reads=[tbk[bk], t_y], writes=[t_y])
                        uh, t_uh = uhs[t % 2], t_uhs[t % 2]
                        uhn, t_uhn = uhs[(t + 1) % 2], t_uhs[(t + 1) % 2]
                        if t + 1 < NTT:
                            for (ybuf, t_y, bk, cc, w0, w1, w2) in hv:
                                cp("dve", uhn[:, cc, :], banks[bk][:, TT - 2:TT], reads=[tbk[bk]], writes=[t_uhn[cc]])
                        if t > 0:
                            for (ybuf, t_y, bk, cc, w0, w1, w2) in hv:
                                stt("dve", ybuf[:, 0:1], uh[:, cc, 1:2], w1, ybuf[:, 0:1], ALU.mult, ALU.add,
                                    reads=[t_uh[cc], t_y], writes=[t_y])
                            for (ybuf, t_y, bk, cc, w0, w1, w2) in hv:
                                stt("dve", ybuf[:, 0:2], uh[:, cc, 0:2], w0, ybuf[:, 0:2], ALU.mult, ALU.add,
                                    reads=[t_uh[cc], t_y], writes=[t_y])
                        act(yg[:], yg[:], AF.Gelu_apprx_tanh, reads=[t_yg], writes=[t_yg])
                        tt_("pool", gT[:, j, :], yg[:], yv[:], ALU.mult, reads=[t_yg, t_yv], writes=[t_gT[j]])
                    return f
                units += [u_up(j) for j in range(NJ)]
                return units

            def down_units(t):
                h3 = t % 3
                gT, t_gT = gTs[t % 2], t_gTs[t % 2]
                units = []

                def u_down(m):
                    def f():
                        k = t * 8 + m
                        wd_issue_upto(k + 1)
                        b = k % NWD
                        bk = 6
                        for j in range(NJ):
                            mm(banks[bk][:, :], wd[b][:, j, :], gT[:, j, :], j == 0, j == NJ - 1,
                               reads=[t_wd[b], t_gT[j]], writes=[tbk[bk]])
                        stt("dve", h1[h3][:, m, :], h1[h3][:, m, :], ALPHA, banks[bk][:, :], ALU.mult, ALU.add,
                            reads=[t_h1[h3][m], tbk[bk]], writes=[t_h1[h3][m]])
                    return f
                units += [u_down(m) for m in range(8)]
                return units

            def L2_units(t):
                hb = t % 3
                tsl = slice(t * TT, (t + 1) * TT)
                units = ln_units(h1[hb], t_h1[hb], C_L2G, C_L2B, h1[hb], t_h1[hb], None, None,
                                 lambda m: at[:, m, :], t_at, lambda m: sq[:, m, :], t_sq,
                                 stA, t_stA, tmpA, t_tmpA, 0, 1)

                def u_store():
                    for m in range(8):
                        dma("sp", yT_v[:, m, tsl], h1[hb][:, m, :], f"out{m}", reads=[t_h1[hb][m]])
                units.append(u_store)
                return units

            for u in F_units(0):
                u()
            wd_issue_upto(0)
            DPOS = [2, 5, 9, 13, 16, 20, 24, 27]
            A = []
            base = {}
            for t in range(NTT + 1):
                base[t] = len(A)
                ups = up_units(t) if t < NTT else []
                downs = down_units(t - 1) if t >= 1 else []
                if ups and downs:
                    ui = di = 0
                    for k in range(30):
                        if di < 8 and k == DPOS[di]:
                            A.append(downs[di]); di += 1
                        else:
                            A.append(ups[ui]); ui += 1
                else:
                    A += ups + downs
            Bs = []
            pos_floor = 0
            for t in range(NTT + 1):
                FOFF = [0, 0, 1, 3, 4, 6, 6, 7, 7, 8, 8, 9, 9, 10, 11, 12, 13, 15, 16, 17, 17, 18, 18]
                LOFF = [0, 1, 2, 3, 5, 6, 7, 7, 8, 8, 9]
                if t + 1 < NTT:
                    fu = F_units(t + 1)
                    assert len(fu) == len(FOFF)
                    p0 = max(base[t], pos_floor)
                    for i, u in enumerate(fu):
                        Bs.append((p0 + FOFF[i], u))
                    pos_floor = p0 + FOFF[-1] + 1
                if t >= 1:
                    lu = L2_units(t - 1)
                    assert len(lu) == len(LOFF)
                    p0 = max(base[t] + (27 if t < NTT else 7), pos_floor)
                    for i, u in enumerate(lu):
                        Bs.append((p0 + LOFF[i], u))
                    pos_floor = p0 + LOFF[-1] + 1
            bi = 0
            for ai, a in enumerate(A):
                a()
                while bi < len(Bs) and Bs[bi][0] <= ai:
                    Bs[bi][1]()
                    bi += 1
            while bi < len(Bs):
                Bs[bi][1]()
                bi += 1
          P.op("sp", None, deps=list(P.lane_last.values()))
          P.op("pool", None, deps=list(P.lane_last.values()))
          P.emit()
    return nc


def _perm_a():
    idx = []
    for c in range(4):
        idx += list(range(c * 64, c * 64 + 64)) + list(range((c + 4) * 64, (c + 4) * 64 + 64))
    return np.array(idx)


def _etab():
    k = np.arange(128)[:, None].astype(np.float64)
    q = np.arange(256)[None, :].astype(np.float64)
    dist = q - k
    et = np.zeros((128, 32, 512), np.float32)

    def tab(slope, md):
        m = (dist >= 0) & (dist <= md)
        return (np.where(m, np.exp(-slope * np.maximum(dist, 0)), 0.0),
                np.where(m, -8.0 * slope * np.maximum(dist, 0), -30000.0))
    for p in range(4):
        for h, head in enumerate((p, p + 4)):
            e, bsl = tab(2.0 ** (-(head + 1)), 127)
            et[:, p, h * 256:(h + 1) * 256] = e
            et[:, 16 + p, h * 256:(h + 1) * 256] = bsl
    for j in range(4):
        for pi, r in enumerate((1, 4, 16)):
            for h, head in enumerate((2 * j, 2 * j + 1)):
                e, bsl = tab(2.0 ** (-(head + 1)) * r, 128)
                et[:, 4 + j * 3 + pi, h * 256:(h + 1) * 256] = e
                et[:, 16 + 4 + j * 3 + pi, h * 256:(h + 1) * 256] = bsl
    return et


_NC_CACHE = {}


def kernel(x, w_in, norm_a_g, norm_b_g, sinks_a, w_o, ln1_g, ln1_b, w_up, conv_w, conv_b, w_down, ln2_g, ln2_b):
    x = np.asarray(x, np.float32)
    w_in = np.asarray(w_in, np.float32)
    w_o = np.asarray(w_o, np.float32)
    w_up = np.asarray(w_up, np.float32)
    w_down = np.asarray(w_down, np.float32)
    pa = _perm_a()
    cols = np.concatenate([pa, np.arange(512, 2304)])
    w_in_p = np.ascontiguousarray(w_in[:, cols])
    rows = np.concatenate([pa, np.arange(512, 1024)])
    w_o_p = np.ascontiguousarray(w_o[rows, :])
    ucols = np.concatenate([np.concatenate([np.arange(j * 128, (j + 1) * 128),
                                            np.arange(DFF + j * 128, DFF + (j + 1) * 128)]) for j in range(NJ)])
    w_up_p = np.ascontiguousarray(w_up[:, ucols])
    cst = np.zeros((128, NCST), np.float32)
    g_att = np.concatenate([np.asarray(norm_a_g, np.float32)[pa], np.asarray(norm_b_g, np.float32)])
    cst[:, C_GATT:C_GATT + 8] = g_att.reshape(8, 128).T
    cst[:, C_L1G:C_L1G + 8] = np.asarray(ln1_g, np.float32).reshape(8, 128).T
    cst[:, C_L1B:C_L1B + 8] = np.asarray(ln1_b, np.float32).reshape(8, 128).T
    cst[:, C_L2G:C_L2G + 8] = np.asarray(ln2_g, np.float32).reshape(8, 128).T
    cst[:, C_L2B:C_L2B + 8] = np.asarray(ln2_b, np.float32).reshape(8, 128).T
    cst[:, C_CB:C_CB + 2 * NJ] = np.asarray(conv_b, np.float32).reshape(2 * NJ, 128).T
    cw = np.asarray(conv_w, np.float32).reshape(3, 2 * NJ, 128)
    cst[:, C_CW:C_CW + 6 * NJ] = cw.transpose(2, 1, 0).reshape(128, 6 * NJ)
    cst[:, C_SINK:C_SINK + 8] = np.asarray(sinks_a, np.float32)[None, :]
    etab = _etab()

    nc = build_nc()
    in_maps = []
    for b in range(8):
        in_maps.append({"xT": np.ascontiguousarray(x[b].T), "w_in": w_in_p, "w_o": w_o_p, "w_up": w_up_p,
                        "w_down": w_down, "cst": cst, "etab": etab})
    res = run_bass_kernel_spmd(nc, in_maps, core_ids=list(range(8)))
    out = np.stack([np.ascontiguousarray(res.results[b]["yT"].T) for b in range(8)], axis=0)
    return out.astype(np.float32)
```

```python
import numpy as np
from contextlib import ExitStack
import concourse.bass as bass
import concourse.mybir as mybir
from concourse.bass_utils import run_bass_kernel_spmd

F32 = mybir.dt.float32
BF16 = mybir.dt.bfloat16
AF = mybir.ActivationFunctionType
ALU = mybir.AluOpType

S = 4096
D = 1024
DFF = 2816
NJ = DFF // 128
TT = 512
NTT = S // TT
ALPHA = 2.0 ** 0.25
LN_EPS = 1e-5
RMS_EPS = 1e-6
NCST = 224
C_GATT, C_L1G, C_L1B, C_L2G, C_L2B, C_CB, C_CW, C_SINK = 0, 8, 16, 24, 32, 40, 84, 216

ENGS = ("pe", "act", "dve", "pool", "sp")
PAIRS = list(range(8))
SKIP_NORM = False
DBG_LEVEL = 9


class Tok:
    __slots__ = ("writers", "readers", "war")

    def __init__(self):
        self.writers = []
        self.readers = []
        self.war = []


class Op:
    __slots__ = ("eng", "fn", "deps", "lane", "needed", "signal", "is_dma", "seq")

    def __init__(self, eng, fn, deps, lane, seq):
        self.eng = eng
        self.fn = fn
        self.deps = deps
        self.lane = lane
        self.is_dma = lane is not None
        self.needed = False
        self.signal = None
        self.seq = seq


class Prog:
    def __init__(self, nc):
        self.nc = nc
        self.ops = {e: [] for e in ENGS}
        self.seq = 0
        self.last_real = {}
        self.lane_last = {}

    def op(self, eng, fn, reads=(), writes=(), deps=(), lane=None):
        d = list(deps)
        for t in reads:
            d.extend(t.writers)
        for t in writes:
            d.extend(t.readers if t.readers else t.war)
        self.seq += 1
        o = Op(eng, fn, d, lane, self.seq)
        for t in reads:
            t.readers.append(o)
        for t in writes:
            if t.readers:
                t.war = t.readers
                t.readers = []
                t.writers = [o]
            else:
                t.writers.append(o)
        self.ops[eng].append(o)
        if fn is not None:
            if lane is not None:
                self.lane_last[lane] = o
            else:
                self.last_real[eng] = o
        return o

    def barrier(self):
        deps = list(self.last_real.values()) + list(self.lane_last.values())
        for e in ENGS:
            self.op(e, None, deps=deps)

    def emit(self):
        nc = self.nc
        for e in ENGS:
            for o in self.ops[e]:
                best = {}
                for d in o.deps:
                    if d is o:
                        continue
                    if (not d.is_dma) and d.eng == "pe" and o.eng == "pe" and not o.is_dma:
                        continue
                    key = ("L", d.lane) if d.is_dma else ("E", d.eng)
                    if key not in best or best[key].seq < d.seq:
                        best[key] = d
                o.deps = list(best.values())
                for d in o.deps:
                    d.needed = True
        with ExitStack() as es:
            sems = {e: es.enter_context(nc.semaphore("s_" + e)) for e in ENGS}
            counters = {e: 0 for e in ENGS}
            lanes = {}
            lanecnt = {}
            for e in ENGS:
                for o in self.ops[e]:
                    if o.fn is None:
                        continue
                    if o.is_dma:
                        if o.lane not in lanes:
                            lanes[o.lane] = es.enter_context(nc.semaphore("l_" + o.lane))
                            lanecnt[o.lane] = 0
                        lanecnt[o.lane] += 16
                        o.signal = (lanes[o.lane], lanecnt[o.lane])
                    elif o.needed:
                        counters[e] += 1
                        o.signal = (sems[e], counters[e])
            block = es.enter_context(nc.Block())

            def run(e, engine):
                waited = {}
                for o in self.ops[e]:
                    for d in o.deps:
                        s, v = d.signal
                        k = id(s)
                        if waited.get(k, 0) < v:
                            engine.wait_ge(s, v)
                            waited[k] = v
                    if o.fn is None:
                        continue
                    inst = o.fn(engine)
                    if o.is_dma:
                        inst.then_inc(o.signal[0], 16)
                    elif o.needed:
                        inst.then_inc(o.signal[0], 1)

            @block.tensor
            def _(eng):
                run("pe", eng)

            @block.scalar
            def _(eng):
                run("act", eng)

            @block.vector
            def _(eng):
                run("dve", eng)

            @block.gpsimd
            def _(eng):
                run("pool", eng)

            @block.sync
            def _(eng):
                run("sp", eng)


def build_nc(upto="C", debug=False):
    nc = bass.Bass("TRN2", target_bir_lowering=False)
    xT = nc.dram_tensor("xT", [D, S], F32, kind="ExternalInput").ap()
    w_in = nc.dram_tensor("w_in", [D, 2304], F32, kind="ExternalInput").ap()
    w_o = nc.dram_tensor("w_o", [D, D], F32, kind="ExternalInput").ap()
    w_up = nc.dram_tensor("w_up", [D, 2 * DFF], F32, kind="ExternalInput").ap()
    w_down = nc.dram_tensor("w_down", [DFF, D], F32, kind="ExternalInput").ap()
    cst_d = nc.dram_tensor("cst", [128, NCST], F32, kind="ExternalInput").ap()
    etab_d = nc.dram_tensor("etab", [128, 32, 512], F32, kind="ExternalInput").ap()
    yT = nc.dram_tensor("yT", [D, S], F32, kind="ExternalOutput").ap()
    ikind = "ExternalOutput" if debug else "Internal"
    projT_d = nc.dram_tensor("projT_d", [18, 128, S], BF16, kind=ikind).ap()
    accT_d = nc.dram_tensor("accT_d", [8, 128, S], BF16, kind=ikind).ap()
    wup_d = nc.dram_tensor("wup_d", [128, 8, 2 * DFF], BF16, kind="Internal").ap()
    wo_d = nc.dram_tensor("wo_d", [128, 8, D], BF16, kind="Internal").ap()
    wdn_d = nc.dram_tensor("wdn_d", [128, 8, NJ, 128], BF16, kind="Internal").ap()

    xT_v = xT.rearrange("(kc p) t -> p kc t", p=128)
    yT_v = yT.rearrange("(kc p) t -> p kc t", p=128)
    w_in_v = w_in.rearrange("(kc p) c -> p kc c", p=128)
    w_o_v = w_o.rearrange("(kc p) c -> p kc c", p=128)
    w_up_v = w_up.rearrange("(kc p) c -> p kc c", p=128)
    w_down_v = w_down.rearrange("(j p) c -> p j c", p=128)
    projT_v = projT_d.rearrange("c p t -> p c t")
    accT_v = accT_d.rearrange("c p t -> p c t")

    P = Prog(nc)

    def dma(eng, out, in_, lane, reads=(), writes=()):
        return P.op(eng, lambda e, o=out, i=in_: e.dma_start(out=o, in_=i), reads, writes, lane=lane)

    def mm(out, lhsT, rhs, start, stop, reads=(), writes=()):
        return P.op("pe", lambda e, o=out, l=lhsT, r=rhs, s=start, t=stop:
                    e.matmul(o, lhsT=l, rhs=r, start=s, stop=t), reads, writes)

    def act(out, in_, func, reads=(), writes=(), scale=None, bias=None):
        def f(e, o=out, i=in_, fn=func, sc=scale, bi=bias):
            kw = {}
            if sc is not None:
                kw["scale"] = sc
            if bi is not None:
                kw["bias"] = bi
            return e.activation(out=o, in_=i, func=fn, **kw)
        return P.op("act", f, reads, writes)

    def tt_(eng, out, in0, in1, op, reads=(), writes=()):
        return P.op(eng, lambda e, o=out, a=in0, b=in1, p=op: e.tensor_tensor(out=o, in0=a, in1=b, op=p),
                    reads, writes)

    def stt(eng, out, in0, scalar, in1, op0, op1, reads=(), writes=()):
        return P.op(eng, lambda e, o=out, a=in0, s=scalar, b=in1, p0=op0, p1=op1:
                    e.scalar_tensor_tensor(out=o, in0=a, scalar=s, in1=b, op0=p0, op1=p1), reads, writes)

    def ts(eng, out, in0, s1, s2, op0, op1, reads=(), writes=()):
        return P.op(eng, lambda e, o=out, a=in0, x=s1, y=s2, p0=op0, p1=op1:
                    e.tensor_scalar(out=o, in0=a, scalar1=x, scalar2=y, op0=p0, op1=p1), reads, writes)

    def cp(eng, out, in_, reads=(), writes=()):
        if eng == "act":
            return act(out, in_, AF.Copy, reads, writes)
        return P.op(eng, lambda e, o=out, i=in_: e.tensor_copy(out=o, in_=i), reads, writes)

    def mset(eng, ap, val, writes=()):
        return P.op(eng, lambda e, a=ap, v=val: e.memset(a, v), (), writes)

    with ExitStack() as es0:
        def sb(es, name, shape, dt):
            return es.enter_context(nc.sbuf_tensor("sb_" + name, shape, dt))

        pall = es0.enter_context(nc.psum_tensor("pall", [128, 8, 512], F32))
        banks = [pall[:, i, :] for i in range(8)]
        bt = pall[:, 7, :].bitcast(BF16)
        tbk = [Tok() for _ in range(8)]
        tbt = tbk[7]

        cst = sb(es0, "cst", [128, NCST], F32)
        es_sb = sb(es0, "es_sb", [128, 8], F32)
        identf = sb(es0, "identf", [128, 128], F32)
        ident = sb(es0, "ident", [128, 128], BF16)
        ones = sb(es0, "ones", [128, 128], BF16)
        epsl = sb(es0, "epsl", [128, 2], F32)
        t_cst, t_es, t_id, t_ones, t_eps = Tok(), Tok(), Tok(), Tok(), Tok()
        t_wup = [Tok() for _ in range(8)]
        t_wod, t_wdn = Tok(), Tok()

        dma("sp", cst[:], cst_d, "cst", writes=[t_cst])
        act(es_sb[:], cst[:, C_SINK:C_SINK + 8], AF.Exp, reads=[t_cst], writes=[t_es])
        mset("pool", identf[:], 0.0, writes=[t_id])
        P.op("pool", lambda e: e.affine_select(out=identf[:], in_=identf[:], pattern=[[-1, 128]],
                                               compare_op=ALU.not_equal, fill=1.0, base=0,
                                               channel_multiplier=1), reads=[t_id], writes=[t_id])
        cp("dve", ident[:], identf[:], reads=[t_id], writes=[t_id])
        mset("dve", ones[:], 1.0, writes=[t_ones])
        mset("dve", epsl[:, 0:1], LN_EPS, writes=[t_eps])
        mset("dve", epsl[:, 1:2], RMS_EPS, writes=[t_eps])

        t_proj = [Tok() for _ in range(18)]
        with ExitStack() as esA:
            w_in_bf = sb(esA, "w_in_bf", [128, 8, 2304], BF16)
            xb = [sb(esA, f"xb{i}", [128, 8, TT], BF16) for i in range(2)]
            pj = [sb(esA, f"pj{i}", [128, 18, TT], BF16) for i in range(2)]
            t_win = [Tok() for _ in range(8)]
            t_xb = [Tok(), Tok()]
            t_pj = [Tok(), Tok()]
            dma("pool", xb[0][:], xT_v[:, :, 0:TT], "xb0", writes=[t_xb[0]])
            for kc in range(8):
                dma("pool", w_in_bf[:, kc, :], w_in_v[:, kc, :], f"win{kc}", writes=[t_win[kc]])
            dma("pool", xb[1][:], xT_v[:, :, TT:2 * TT], "xb1", writes=[t_xb[1]])
            for tti in range(NTT):
                b = tti % 2
                if 1 <= tti and tti + 1 < NTT:
                    nb_ = (tti + 1) % 2
                    dma("pool", xb[nb_][:], xT_v[:, :, (tti + 1) * TT:(tti + 2) * TT], f"xb{nb_}", writes=[t_xb[nb_]])
                dma("pool", wup_d[:, tti, :], w_up_v[:, tti, :], "wupc", writes=[t_wup[tti]])
                dma("pool", wdn_d[:, tti, :, :], w_down_v[:, :, tti * 128:(tti + 1) * 128], "wdnc", writes=[t_wdn])
                if tti == 0:
                    dma("pool", wo_d, w_o_v, "woc", writes=[t_wod])
                for c0 in range(0, 18, 2):
                    for kc in range(8):
                        for c in (c0, c0 + 1):
                            bk = c % 4
                            mm(banks[bk][:, :], w_in_bf[:, kc, c * 128:(c + 1) * 128], xb[b][:, kc, :],
                               kc == 0, kc == 7, reads=[t_win[kc], t_xb[b]], writes=[tbk[bk]])
                    for c in (c0, c0 + 1):
                        bk = c % 4
                        cp("act" if c % 2 == 0 else "dve", pj[b][:, c, :], banks[bk][:, :],
                           reads=[tbk[bk]], writes=[t_pj[b]])
                dma("sp", projT_v[:, :, tti * TT:(tti + 1) * TT], pj[b][:], f"pj{b}",
                    reads=[t_pj[b]], writes=t_proj)
        P.barrier()

        t_acc_d = [Tok() for _ in range(8)]
        with ExitStack() as esB:
          if upto in ("B", "C"):
            epair = sb(esB, "epair", [128, 32, 512], BF16)
            qT = [sb(esB, f"qT{i}", [128, S], BF16) for i in range(2)]
            kT = [sb(esB, f"kT{i}", [128, S], BF16) for i in range(2)]
            vT = [sb(esB, f"vT{i}", [128, S], BF16) for i in range(2)]
            vaug = [sb(esB, f"vaug{i}", [128, 32, 2, 128], BF16) for i in range(2)]
            acc32 = sb(esB, "acc32", [128, 2, S], F32)
            NPT = 5
            pt = [sb(esB, f"pt{i}", [128, 2, 256], BF16) for i in range(NPT)]
            lnl = [sb(esB, f"lnl{i}", [128, 1024], F32) for i in range(2)]
            rec = [sb(esB, f"rec{i}", [128, 1024], F32) for i in range(2)]
            oT = [sb(esB, f"oT{i}", [128, S], BF16) for i in range(2)]
            t_ep = Tok()
            t_q = [Tok(), Tok()]
            t_k = [Tok(), Tok()]
            t_v = [Tok(), Tok()]
            t_va = [[Tok() for _ in range(4)] for _ in range(2)]
            t_acc = [Tok() for _ in range(32)]
            t_pt = [Tok() for _ in range(NPT)]
            t_lnl = [Tok(), Tok()]
            t_rec = [Tok(), Tok()]
            t_oT = [Tok(), Tok()]

            for h4 in range(8):
                dma("pool", epair[:, 4 * h4:4 * h4 + 4, :], etab_d[:, 4 * h4:4 * h4 + 4, :], "etab", writes=[t_ep])
            for i in range(2):
                mset("pool", vaug[i][:, :, :, 64:128], 1.0, writes=t_va[i])

            def pair_chunks(p):
                if p < 4:
                    return p, 4, 5
                j = p - 4
                return 6 + j, 10 + j, 14 + j

            def issue_loads(p):
                cq, ck, cv = pair_chunks(p)
                qi = p % 2
                dma("sp", qT[qi][:], projT_d[cq], f"q{qi}", reads=[t_proj[cq]], writes=[t_q[qi]])
                if p == PAIRS[0] or p >= 4:
                    ki = 0 if p < 4 else (p + 1) % 2
                    dma("sp", kT[ki][:], projT_d[ck], f"k{ki}", reads=[t_proj[ck]], writes=[t_k[ki]])
                    dma("sp", vT[ki][:], projT_d[cv], f"v{ki}", reads=[t_proj[cv]], writes=[t_v[ki]])

            issue_loads(PAIRS[0])
            npat = 0
            bseq = 0
            oseq = 0
            for pidx, p in enumerate(PAIRS):
                if pidx + 1 < len(PAIRS):
                    issue_loads(PAIRS[pidx + 1])
                qi = p % 2
                ki = 0 if p < 4 else (p + 1) % 2
                pats = [(1, 0)] if p < 4 else [(1, 0), (4, 1), (16, 2)]
                for (r, pi) in pats:
                    ti = p if p < 4 else 4 + (p - 4) * 3 + pi
                    nb = 32 // r
                    reuse_v = (0 < p < 4) and PAIRS[0] == 0
                    if reuse_v:
                        va, tva = vaug[0], t_va[0]
                    else:
                        va = vaug[npat % 2]
                        tva = t_va[npat % 2]
                        npat += 1
                    blocks = [(rho, n) for rho in range(r) for n in range(nb)]

                    def tstart(rho, n):
                        return rho + r * 128 * n

                    for g in range(4 if (DBG_LEVEL >= 1 and not reuse_v) else 0):
                        for i in range(8):
                            rho, n = blocks[8 * g + i]
                            st = tstart(rho, n)
                            P.op("pe", lambda e, o=bt[:, i * 128:(i + 1) * 128],
                                 a=vT[ki][:, st:st + r * 127 + 1:r]: e.transpose(o, a, ident[:]),
                                 reads=[t_v[ki], t_id], writes=[tbt])
                        cp("act", va[:, 8 * g:8 * g + 8, :, 0:64],
                           bt[:, :].rearrange("p (b h d) -> p b h d", b=8, h=2),
                           reads=[tbt], writes=[tva[g]])

                    info = {}

                    def emit_scores(bi):
                        nonlocal bseq
                        rho, n = blocks[bi]
                        st = tstart(rho, n)
                        last = (n == nb - 1)
                        nq = 128 if last else 256
                        sbk = 2 * (bseq % 2)
                        pti = bseq % NPT
                        bseq += 1
                        info[bi] = pti
                        use_bias = False
                        if use_bias:
                            for h in range(2):
                                mm(banks[sbk + h][:, 0:nq], ident[:], epair[:, 16 + ti, h * 256:h * 256 + nq],
                                   True, False, reads=[t_id, t_ep], writes=[tbk[sbk + h]])
                        for h in range(2):
                            mm(banks[sbk + h][:, 0:nq],
                               kT[ki][h * 64:(h + 1) * 64, st:st + r * 127 + 1:r],
                               qT[qi][h * 64:(h + 1) * 64, st:st + r * (nq - 1) + 1:r],
                               not use_bias, True, reads=[t_k[ki], t_q[qi]], writes=[tbk[sbk + h]])
                        sv = pall[:, sbk:sbk + 2, 0:nq]
                        act(pt[pti][:, :, 0:nq], sv, AF.Exp, reads=[tbk[sbk], tbk[sbk + 1]], writes=[t_pt[pti]],
                            scale=0.125)
                        if not use_bias:
                            ev = epair[:, ti, :].rearrange("p (h q) -> p h q", h=2)[:, :, 0:nq]
                            tt_("dve", pt[pti][:, :, 0:nq], pt[pti][:, :, 0:nq], ev, ALU.mult,
                                reads=[t_pt[pti], t_ep], writes=[t_pt[pti]])

                    def emit_pv(bi, first):
                        nonlocal oseq
                        rho, n = blocks[bi]
                        obk = 4 + ((oseq // 2) % 3)
                        oseq += 1
                        pc = info[bi]
                        blk = bi
                        for h in range(2):
                            c0 = h * 256 + (n % 2) * 128
                            o_ap = banks[obk][:, c0:c0 + 128]
                            if n > 0:
                                pp = info[bi - 1]
                                mm(o_ap, va[:, blk - 1, h, :], pt[pp][:, h, 128:256], True, False,
                                   reads=[tva[(blk - 1) // 8], t_pt[pp]], writes=[tbk[obk]])
                                mm(o_ap, va[:, blk, h, :], pt[pc][:, h, 0:128], False, True,
                                   reads=[tva[blk // 8], t_pt[pc]], writes=[tbk[obk]])
                            else:
                                mm(o_ap, va[:, blk, h, :], pt[pc][:, h, 0:128], True, True,
                                   reads=[tva[blk // 8], t_pt[pc]], writes=[tbk[obk]])
                        if n % 2 == 0:
                            return
                        st0 = tstart(rho, n - 1)
                        ov = banks[obk][:, :].rearrange("p (h q) -> p h q", h=2)
                        av = acc32[:, :, st0:st0 + r * 255 + 1:r]
                        nb0 = st0 // 128
                        nb1 = (st0 + r * 255) // 128
                        toks = t_acc[nb0:nb1 + 1]
                        if first:
                            cp("act", av, ov, reads=[tbk[obk]], writes=toks)
                        else:
                            tt_("dve", av, ov, av, ALU.add, reads=[tbk[obk]] + toks, writes=toks)

                    LOOK = 2
                    for i in range(len(blocks) + LOOK):
                        if i < len(blocks) and DBG_LEVEL >= 2:
                            emit_scores(i)
                        if i - LOOK >= 0 and DBG_LEVEL >= 3:
                            emit_pv(i - LOOK, pi == 0)

                oi = p % 2
                for h in range(0 if not SKIP_NORM else 2, 2):
                    head = (p + 4 * h) if p < 4 else 0
                    for c4 in range(4):
                        sl = slice(c4 * 1024, (c4 + 1) * 1024)
                        toks = t_acc[c4 * 8:(c4 + 1) * 8]
                        li = (h * 4 + c4) % 2
                        bias = es_sb[64:128, head:head + 1] if p < 4 else None
                        act(lnl[li][0:64, :], acc32[64:128, h, sl], AF.Ln,
                            reads=toks + [t_es], writes=[t_lnl[li]], bias=bias)
                        act(rec[li][0:64, :], lnl[li][0:64, :], AF.Exp, reads=[t_lnl[li]],
                            writes=[t_rec[li]], scale=-1.0)
                        tt_("dve", oT[oi][h * 64:(h + 1) * 64, sl], acc32[0:64, h, sl], rec[li][0:64, :], ALU.mult,
                            reads=toks + [t_rec[li]], writes=[t_oT[oi]])
                dma("sp", accT_d[p], oT[oi][:], f"oT{oi}", reads=[t_oT[oi]], writes=[t_acc_d[p]])
        P.barrier()

        with ExitStack() as esC:
          if upto == "C":
            wo_bf = sb(esC, "wo_bf", [128, 8, D], BF16)
            NWD = 2
            wd = [sb(esC, f"wd{i}", [128, NJ, 128], BF16) for i in range(NWD)]
            wu = [sb(esC, f"wu{i}", [128, 8, 512], BF16) for i in range(2)]
            at = sb(esC, "at", [128, 8, TT], BF16)
            sq = sb(esC, "sq", [128, 8, TT], BF16)
            x32 = sb(esC, "x32", [128, 8, TT], F32)
            h1 = [sb(esC, f"h1_{i}", [128, 8, TT], F32) for i in range(3)]
            h1b = [sb(esC, f"h1b_{i}", [128, 8, TT], BF16) for i in range(2)]
            gTs = [sb(esC, f"gT{i}", [128, NJ, TT], BF16) for i in range(2)]
            ygs = [sb(esC, f"yg{i}", [128, TT], F32) for i in range(2)]
            yvs = [sb(esC, f"yv{i}", [128, TT], F32) for i in range(2)]
            tmpA = [sb(esC, f"tmpA{i}", [128, TT], F32) for i in range(2)]
            stA = [sb(esC, f"stA{i}", [128, TT], F32) for i in range(4)]
            uhs = [sb(esC, f"uh{i}", [128, 2 * NJ, 2], F32) for i in range(2)]
            t_wo = Tok()
            t_wd = [Tok() for _ in range(NWD)]
            t_wu = [Tok(), Tok()]
            t_at = [Tok() for _ in range(8)]
            t_sq = [Tok() for _ in range(8)]
            t_x32 = [Tok() for _ in range(8)]
            t_h1 = [[Tok() for _ in range(8)] for _ in range(3)]
            t_h1b = [[Tok() for _ in range(8)] for _ in range(2)]
            t_gTs = [[Tok() for _ in range(NJ)] for _ in range(2)]
            t_ygs, t_yvs = [Tok(), Tok()], [Tok(), Tok()]
            t_tmpA = [Tok(), Tok()]
            t_stA = [Tok() for _ in range(4)]
            t_uhs = [[Tok() for _ in range(2 * NJ)] for _ in range(2)]

            dma("sp", wo_bf[:], wo_d, "wo", reads=[t_wod], writes=[t_wo])

            def col(c):
                return cst[:, c:c + 1]

            def ln_units(src, t_src, gcol, bcol, dst32, t_d32, dstbf, t_dbf, cpb, t_cpb, sqb, t_sqb,
                         st, t_st, tmp, t_tmp, bkA, bkB):
                mean, mv, lr, nmr = st
                units = []

                def u_sq(m0):
                    def f():
                        for m in range(m0, m0 + 2):
                            cp("dve", cpb(m), src[:, m, :], reads=[t_src[m]], writes=[t_cpb[m]])
                            act(sqb(m), src[:, m, :], AF.Square, reads=[t_src[m]], writes=[t_sqb[m]])
                    return f
                units += [u_sq(0), u_sq(2), u_sq(4), u_sq(6)]

                def u_mm():
                    for m in range(8):
                        mm(banks[bkA][:, :], ones[:], cpb(m), m == 0, m == 7, reads=[t_ones, t_cpb[m]], writes=[tbk[bkA]])
                    for m in range(8):
                        mm(banks[bkB][:, :], ones[:], sqb(m), m == 0, m == 7, reads=[t_ones, t_sqb[m]], writes=[tbk[bkB]])
                units.append(u_mm)

                def u_stats():
                    act(mean[:], banks[bkA][:, :], AF.Copy, reads=[tbk[bkA]], writes=[t_st[0]], scale=1.0 / D)
                    act(mv[:], banks[bkA][:, :], AF.Square, reads=[tbk[bkA]], writes=[t_st[1]], scale=1.0 / D)
                    stt("dve", mv[:], banks[bkB][:, :], 1.0 / D, mv[:], ALU.mult, ALU.subtract,
                        reads=[tbk[bkB], t_st[1]], writes=[t_st[1]])
                    act(lr[:], mv[:], AF.Ln, reads=[t_st[1], t_eps], writes=[t_st[2]], bias=epsl[:, 0:1])
                    act(lr[:], lr[:], AF.Exp, reads=[t_st[2]], writes=[t_st[2]], scale=-0.5)
                    stt("dve", nmr[:], mean[:], -1.0, lr[:], ALU.mult, ALU.mult,
                        reads=[t_st[0], t_st[2]], writes=[t_st[3]])
                units.append(u_stats)

                def u_norm(m0):
                    def f():
                        for m in range(m0, m0 + 2):
                            tb = m % 2
                            tt_("dve", tmp[tb][:], src[:, m, :], lr[:], ALU.mult, reads=[t_src[m], t_st[2]], writes=[t_tmp[tb]])
                        for m in range(m0, m0 + 2):
                            tb = m % 2
                            tt_("dve", tmp[tb][:], tmp[tb][:], nmr[:], ALU.add, reads=[t_tmp[tb], t_st[3]], writes=[t_tmp[tb]])
                        for m in range(m0, m0 + 2):
                            tb = m % 2
                            ts("pool", dst32[:, m, :], tmp[tb][:], col(gcol + m), col(bcol + m), ALU.mult, ALU.add,
                               reads=[t_tmp[tb], t_cst], writes=[t_d32[m]])
                            if dstbf is not None:
                                ts("pool", dstbf[:, m, :], tmp[tb][:], col(gcol + m), col(bcol + m), ALU.mult, ALU.add,
                                   reads=[t_tmp[tb], t_cst], writes=[t_dbf[m]])
                    return f
                units += [u_norm(0), u_norm(2), u_norm(4), u_norm(6)]
                return units

            wu_list = [(t, jj) for t in range(NTT) for jj in range(NJ // 2)]
            wu_issued = [0]

            def wu_issue_upto(k):
                while wu_issued[0] <= k and wu_issued[0] < len(wu_list):
                    i = wu_issued[0]
                    b = i % 2
                    jj = wu_list[i][1]
                    dma("sp", wu[b][:], wup_d[:, :, jj * 512:(jj + 1) * 512], f"wu{b}", reads=t_wup, writes=[t_wu[b]])
                    wu_issued[0] += 1

            wd_issued = [0]

            def wd_issue_upto(k):
                while wd_issued[0] <= k and wd_issued[0] < NTT * 8:
                    i = wd_issued[0]
                    b = i % NWD
                    m = i % 8
                    dma("sp", wd[b][:], wdn_d[:, m, :, :], f"wdl{b}", reads=[t_wdn], writes=[t_wd[b]])
                    wd_issued[0] += 1

            def F_units(t):
                hb = t % 2
                h3 = t % 3
                tsl = slice(t * TT, (t + 1) * TT)
                units = []

                def u_load():
                    for m in range(8):
                        dma("sp", at[:, m, :], accT_v[:, m, tsl], f"at{m}", reads=[t_acc_d[m]], writes=[t_at[m]])
                    for m in range(8):
                        dma("sp", x32[:, m, :], xT_v[:, m, tsl], f"x32_{m}", writes=[t_x32[m]])
                units.append(u_load)

                def u_rsq(c0):
                    def f():
                        for c in range(c0, c0 + 4):
                            tt_("dve", sq[:, c, :], at[:, c, :], at[:, c, :], ALU.mult, reads=[t_at[c]], writes=[t_sq[c]])
                    return f
                units += [u_rsq(0), u_rsq(4)]

                def u_rmm():
                    for gi in range(2):
                        for i, c in enumerate(range(4 * gi, 4 * gi + 4)):
                            mm(banks[gi][:, :], ones[:], sq[:, c, :], i == 0, i == 3, reads=[t_ones, t_sq[c]], writes=[tbk[gi]])
                        a = stA[2 * gi]
                        act(a[:], banks[gi][:, :], AF.Ln, reads=[tbk[gi], t_eps], writes=[t_stA[2 * gi]],
                            scale=1.0 / 512, bias=epsl[:, 1:2])
                        act(a[:], a[:], AF.Exp, reads=[t_stA[2 * gi]], writes=[t_stA[2 * gi]], scale=-0.5)
                units.append(u_rmm)

                def u_rn():
                    for gi in range(2):
                        for c in range(4 * gi, 4 * gi + 4):
                            stt("dve", at[:, c, :], at[:, c, :], col(C_GATT + c), stA[2 * gi][:], ALU.mult, ALU.mult,
                                reads=[t_at[c], t_stA[2 * gi], t_cst], writes=[t_at[c]])
                units.append(u_rn)

                def u_wo(m):
                    def f():
                        bk = m % 2
                        for kc in range(8):
                            mm(banks[bk][:, :], wo_bf[:, kc, m * 128:(m + 1) * 128], at[:, kc, :], kc == 0, kc == 7,
                               reads=[t_wo, t_at[kc]], writes=[tbk[bk]])
                        stt("dve", x32[:, m, :], x32[:, m, :], ALPHA, banks[bk][:, :], ALU.mult, ALU.add,
                            reads=[t_x32[m], tbk[bk]], writes=[t_x32[m]])
                    return f
                units += [u_wo(m) for m in range(8)]
                units += ln_units(x32, t_x32, C_L1G, C_L1B, h1[h3], t_h1[h3], h1b[hb], t_h1b[hb],
                                  lambda m: at[:, m, :], t_at, lambda m: sq[:, m, :], t_sq,
                                  stA, t_stA, tmpA, t_tmpA, 0, 1)
                return units

            UPB = [2, 3, 4, 5, 7]

            def up_units(t):
                hb = t % 2
                gT, t_gT = gTs[t % 2], t_gTs[t % 2]
                units = []

                def u_up(j):
                    def f():
                        k = t * (NJ // 2) + j // 2
                        if j % 2 == 0:
                            wu_issue_upto(k + 1)
                        cur = k % 2
                        woff = (j % 2) * 256
                        yg, yv, t_yg, t_yv = ygs[j % 2], yvs[j % 2], t_ygs[j % 2], t_yvs[j % 2]
                        hv = []
                        for kc in range(8):
                            for half in range(2):
                                bk = UPB[(2 * j + half) % 5]
                                mm(banks[bk][:, :], wu[cur][:, kc, woff + half * 128:woff + half * 128 + 128],
                                   h1b[hb][:, kc, :], kc == 0, kc == 7, reads=[t_wu[cur], t_h1b[hb][kc]], writes=[tbk[bk]])
                        for half, (ybuf, t_y) in enumerate(((yg, t_yg), (yv, t_yv))):
                            bk = UPB[(2 * j + half) % 5]
                            cc = j + half * NJ
                            w0, w1, w2 = (col(C_CW + cc * 3 + kk) for kk in range(3))
                            hv.append((ybuf, t_y, bk, cc, w0, w1, w2))
                        for (ybuf, t_y, bk, cc, w0, w1, w2) in hv:
                            act(ybuf[:], banks[bk][:, :], AF.Identity, reads=[tbk[bk], t_cst], writes=[t_y],
                                scale=w2, bias=col(C_CB + cc))
                        for (ybuf, t_y, bk, cc, w0, w1, w2) in hv:
                            stt("dve", ybuf[:, 1:TT], banks[bk][:, 0:TT - 1], w1, ybuf[:, 1:TT], ALU.mult, ALU.add,
                                reads=[tbk[bk], t_y], writes=[t_y])
                        for (ybuf, t_y, bk, cc, w0, w1, w2) in hv:
                            stt("dve", ybuf[:, 2:TT], banks[bk][:, 0:TT - 2], w0, ybuf[:, 2:TT], ALU.mult, ALU.add,
                                reads=[tbk[bk], t_y], writes=[t_y])
                        uh, t_uh = uhs[t % 2], t_uhs[t % 2]
                        uhn, t_uhn = uhs[(t + 1) % 2], t_uhs[(t + 1) % 2]
                        if t + 1 < NTT:
                            for (ybuf, t_y, bk, cc, w0, w1, w2) in hv:
                                cp("dve", uhn[:, cc, :], banks[bk][:, TT - 2:TT], reads=[tbk[bk]], writes=[t_uhn[cc]])
                        if t > 0:
                            for (ybuf, t_y, bk, cc, w0, w1, w2) in hv:
                                stt("dve", ybuf[:, 0:1], uh[:, cc, 1:2], w1, ybuf[:, 0:1], ALU.mult, ALU.add,
                                    reads=[t_uh[cc], t_y], writes=[t_y])
                            for (ybuf, t_y, bk, cc, w0, w1, w2) in hv:
                                stt("dve", ybuf[:, 0:2], uh[:, cc, 0:2], w0, ybuf[:, 0:2], ALU.mult, ALU.add,
                                    reads=[t_uh[cc], t_y], writes=[t_y])
                        act(yg[:], yg[:], AF.Gelu_apprx_tanh, reads=[t_yg], writes=[t_yg])
                        tt_("pool", gT[:, j, :], yg[:], yv[:], ALU.mult, reads=[t_yg, t_yv], writes=[t_gT[j]])
                    return f
                units += [u_up(j) for j in range(NJ)]
                return units

            def down_units(t):
                h3 = t % 3
                gT, t_gT = gTs[t % 2], t_gTs[t % 2]
                units = []

                def u_down(m):
                    def f():
                        k = t * 8 + m
                        wd_issue_upto(k + 1)
                        b = k % NWD
                        bk = 6
                        for j in range(NJ):
                            mm(banks[bk][:, :], wd[b][:, j, :], gT[:, j, :], j == 0, j == NJ - 1,
                               reads=[t_wd[b], t_gT[j]], writes=[tbk[bk]])
                        stt("dve", h1[h3][:, m, :], h1[h3][:, m, :], ALPHA, banks[bk][:, :], ALU.mult, ALU.add,
                            reads=[t_h1[h3][m], tbk[bk]], writes=[t_h1[h3][m]])
                    return f
                units += [u_down(m) for m in range(8)]
                return units

            def L2_units(t):
                hb = t % 3
                tsl = slice(t * TT, (t + 1) * TT)
                units = ln_units(h1[hb], t_h1[hb], C_L2G, C_L2B, h1[hb], t_h1[hb], None, None,
                                 lambda m: at[:, m, :], t_at, lambda m: sq[:, m, :], t_sq,
                                 stA, t_stA, tmpA, t_tmpA, 0, 1)

                def u_store():
                    for m in range(8):
                        dma("sp", yT_v[:, m, tsl], h1[hb][:, m, :], f"out{m}", reads=[t_h1[hb][m]])
                units.append(u_store)
                return units

            for u in F_units(0):
                u()
            wd_issue_upto(0)
            DPOS = [2, 5, 9, 13, 16, 20, 24, 27]
            A = []
            base = {}
            for t in range(NTT + 1):
                base[t] = len(A)
                ups = up_units(t) if t < NTT else []
                downs = down_units(t - 1) if t >= 1 else []
                if ups and downs:
                    ui = di = 0
                    for k in range(30):
                        if di < 8 and k == DPOS[di]:
                            A.append(downs[di]); di += 1
                        else:
                            A.append(ups[ui]); ui += 1
                else:
                    A += ups + downs
            Bs = []
            pos_floor = 0
            for t in range(NTT + 1):
                FOFF = [0, 0, 1, 3, 4, 6, 6, 7, 7, 8, 8, 9, 9, 10, 11, 12, 13, 15, 16, 17, 17, 18, 18]
                LOFF = [0, 1, 2, 3, 5, 6, 7, 7, 8, 8, 9]
                if t + 1 < NTT:
                    fu = F_units(t + 1)
                    assert len(fu) == len(FOFF)
                    p0 = max(base[t], pos_floor)
                    for i, u in enumerate(fu):
                        Bs.append((p0 + FOFF[i], u))
                    pos_floor = p0 + FOFF[-1] + 1
                if t >= 1:
                    lu = L2_units(t - 1)
                    assert len(lu) == len(LOFF)
                    p0 = max(base[t] + (27 if t < NTT else 7), pos_floor)
                    for i, u in enumerate(lu):
                        Bs.append((p0 + LOFF[i], u))
                    pos_floor = p0 + LOFF[-1] + 1
            bi = 0
            for ai, a in enumerate(A):
                a()
                while bi < len(Bs) and Bs[bi][0] <= ai:
                    Bs[bi][1]()
                    bi += 1
            while bi < len(Bs):
                Bs[bi][1]()
                bi += 1
          P.op("sp", None, deps=list(P.lane_last.values()))
          P.op("pool", None, deps=list(P.lane_last.values()))
          P.emit()
    return nc


def _perm_a():
    idx = []
    for c in range(4):
        idx += list(range(c * 64, c * 64 + 64)) + list(range((c + 4) * 64, (c + 4) * 64 + 64))
    return np.array(idx)


def _etab():
    k = np.arange(128)[:, None].astype(np.float64)
    q = np.arange(256)[None, :].astype(np.float64)
    dist = q - k
    et = np.zeros((128, 32, 512), np.float32)

    def tab(slope, md):
        m = (dist >= 0) & (dist <= md)
        return (np.where(m, np.exp(-slope * np.maximum(dist, 0)), 0.0),
                np.where(m, -8.0 * slope * np.maximum(dist, 0), -30000.0))
    for p in range(4):
        for h, head in enumerate((p, p + 4)):
            e, bsl = tab(2.0 ** (-(head + 1)), 127)
            et[:, p, h * 256:(h + 1) * 256] = e
            et[:, 16 + p, h * 256:(h + 1) * 256] = bsl
    for j in range(4):
        for pi, r in enumerate((1, 4, 16)):
            for h, head in enumerate((2 * j, 2 * j + 1)):
                e, bsl = tab(2.0 ** (-(head + 1)) * r, 128)
                et[:, 4 + j * 3 + pi, h * 256:(h + 1) * 256] = e
                et[:, 16 + 4 + j * 3 + pi, h * 256:(h + 1) * 256] = bsl
    return et


_NC_CACHE = {}


def kernel(x, w_in, norm_a_g, norm_b_g, sinks_a, w_o, ln1_g, ln1_b, w_up, conv_w, conv_b, w_down, ln2_g, ln2_b):
    x = np.asarray(x, np.float32)
    w_in = np.asarray(w_in, np.float32)
    w_o = np.asarray(w_o, np.float32)
    w_up = np.asarray(w_up, np.float32)
    w_down = np.asarray(w_down, np.float32)
    pa = _perm_a()
    cols = np.concatenate([pa, np.arange(512, 2304)])
    w_in_p = np.ascontiguousarray(w_in[:, cols])
    rows = np.concatenate([pa, np.arange(512, 1024)])
    w_o_p = np.ascontiguousarray(w_o[rows, :])
    ucols = np.concatenate([np.concatenate([np.arange(j * 128, (j + 1) * 128),
                                            np.arange(DFF + j * 128, DFF + (j + 1) * 128)]) for j in range(NJ)])
    w_up_p = np.ascontiguousarray(w_up[:, ucols])
    cst = np.zeros((128, NCST), np.float32)
    g_att = np.concatenate([np.asarray(norm_a_g, np.float32)[pa], np.asarray(norm_b_g, np.float32)])
    cst[:, C_GATT:C_GATT + 8] = g_att.reshape(8, 128).T
    cst[:, C_L1G:C_L1G + 8] = np.asarray(ln1_g, np.float32).reshape(8, 128).T
    cst[:, C_L1B:C_L1B + 8] = np.asarray(ln1_b, np.float32).reshape(8, 128).T
    cst[:, C_L2G:C_L2G + 8] = np.asarray(ln2_g, np.float32).reshape(8, 128).T
    cst[:, C_L2B:C_L2B + 8] = np.asarray(ln2_b, np.float32).reshape(8, 128).T
    cst[:, C_CB:C_CB + 2 * NJ] = np.asarray(conv_b, np.float32).reshape(2 * NJ, 128).T
    cw = np.asarray(conv_w, np.float32).reshape(3, 2 * NJ, 128)
    cst[:, C_CW:C_CW + 6 * NJ] = cw.transpose(2, 1, 0).reshape(128, 6 * NJ)
    cst[:, C_SINK:C_SINK + 8] = np.asarray(sinks_a, np.float32)[None, :]
    etab = _etab()

    nc = build_nc()
    in_maps = []
    for b in range(8):
        in_maps.append({"xT": np.ascontiguousarray(x[b].T), "w_in": w_in_p, "w_o": w_o_p, "w_up": w_up_p,
                        "w_down": w_down, "cst": cst, "etab": etab})
    res = run_bass_kernel_spmd(nc, in_maps, core_ids=list(range(8)))
    out = np.stack([np.ascontiguousarray(res.results[b]["yT"].T) for b in range(8)], axis=0)
    return out.astype(np.float32)
```

```python
import numpy as np
from contextlib import ExitStack
import concourse.bass as bass
import concourse.mybir as mybir
from concourse.bass_utils import run_bass_kernel_spmd

F32 = mybir.dt.float32
BF16 = mybir.dt.bfloat16
AF = mybir.ActivationFunctionType
ALU = mybir.AluOpType

S = 4096
D = 1024
DFF = 2816
NJ = DFF // 128
TT = 512
NTT = S // TT
ALPHA = 2.0 ** 0.25
LN_EPS = 1e-5
RMS_EPS = 1e-6
NCST = 224
C_GATT, C_L1G, C_L1B, C_L2G, C_L2B, C_CB, C_CW, C_SINK = 0, 8, 16, 24, 32, 40, 84, 216

ENGS = ("pe", "act", "dve", "pool", "sp")
PAIRS = list(range(8))
SKIP_NORM = False
DBG_LEVEL = 9


class Tok:
    __slots__ = ("writers", "readers", "war")

    def __init__(self):
        self.writers = []
        self.readers = []
        self.war = []


class Op:
    __slots__ = ("eng", "fn", "deps", "lane", "needed", "signal", "is_dma", "seq")

    def __init__(self, eng, fn, deps, lane, seq):
        self.eng = eng
        self.fn = fn
        self.deps = deps
        self.lane = lane
        self.is_dma = lane is not None
        self.needed = False
        self.signal = None
        self.seq = seq


class Prog:
    def __init__(self, nc):
        self.nc = nc
        self.ops = {e: [] for e in ENGS}
        self.seq = 0
        self.last_real = {}
        self.lane_last = {}

    def op(self, eng, fn, reads=(), writes=(), deps=(), lane=None):
        d = list(deps)
        for t in reads:
            d.extend(t.writers)
        for t in writes:
            d.extend(t.readers if t.readers else t.war)
        self.seq += 1
        o = Op(eng, fn, d, lane, self.seq)
        for t in reads:
            t.readers.append(o)
        for t in writes:
            if t.readers:
                t.war = t.readers
                t.readers = []
                t.writers = [o]
            else:
                t.writers.append(o)
        self.ops[eng].append(o)
        if fn is not None:
            if lane is not None:
                self.lane_last[lane] = o
            else:
                self.last_real[eng] = o
        return o

    def barrier(self):
        deps = list(self.last_real.values()) + list(self.lane_last.values())
        for e in ENGS:
            self.op(e, None, deps=deps)

    def emit(self):
        nc = self.nc
        for e in ENGS:
            for o in self.ops[e]:
                best = {}
                for d in o.deps:
                    if d is o:
                        continue
                    if (not d.is_dma) and d.eng == "pe" and o.eng == "pe" and not o.is_dma:
                        continue
                    key = ("L", d.lane) if d.is_dma else ("E", d.eng)
                    if key not in best or best[key].seq < d.seq:
                        best[key] = d
                o.deps = list(best.values())
                for d in o.deps:
                    d.needed = True
        with ExitStack() as es:
            sems = {e: es.enter_context(nc.semaphore("s_" + e)) for e in ENGS}
            counters = {e: 0 for e in ENGS}
            lanes = {}
            lanecnt = {}
            for e in ENGS:
                for o in self.ops[e]:
                    if o.fn is None:
                        continue
                    if o.is_dma:
                        if o.lane not in lanes:
                            lanes[o.lane] = es.enter_context(nc.semaphore("l_" + o.lane))
                            lanecnt[o.lane] = 0
                        lanecnt[o.lane] += 16
                        o.signal = (lanes[o.lane], lanecnt[o.lane])
                    elif o.needed:
                        counters[e] += 1
                        o.signal = (sems[e], counters[e])
            block = es.enter_context(nc.Block())

            def run(e, engine):
                waited = {}
                for o in self.ops[e]:
                    for d in o.deps:
                        s, v = d.signal
                        k = id(s)
                        if waited.get(k, 0) < v:
                            engine.wait_ge(s, v)
                            waited[k] = v
                    if o.fn is None:
                        continue
                    inst = o.fn(engine)
                    if o.is_dma:
                        inst.then_inc(o.signal[0], 16)
                    elif o.needed:
                        inst.then_inc(o.signal[0], 1)

            @block.tensor
            def _(eng):
                run("pe", eng)

            @block.scalar
            def _(eng):
                run("act", eng)

            @block.vector
            def _(eng):
                run("dve", eng)

            @block.gpsimd
            def _(eng):
                run("pool", eng)

            @block.sync
            def _(eng):
                run("sp", eng)


def build_nc(upto="C", debug=False):
    nc = bass.Bass("TRN2", target_bir_lowering=False)
    xT = nc.dram_tensor("xT", [D, S], F32, kind="ExternalInput").ap()
    w_in = nc.dram_tensor("w_in", [D, 2304], F32, kind="ExternalInput").ap()
    w_o = nc.dram_tensor("w_o", [D, D], F32, kind="ExternalInput").ap()
    w_up = nc.dram_tensor("w_up", [D, 2 * DFF], F32, kind="ExternalInput").ap()
    w_down = nc.dram_tensor("w_down", [DFF, D], F32, kind="ExternalInput").ap()
    cst_d = nc.dram_tensor("cst", [128, NCST], F32, kind="ExternalInput").ap()
    etab_d = nc.dram_tensor("etab", [128, 32, 512], F32, kind="ExternalInput").ap()
    yT = nc.dram_tensor("yT", [D, S], F32, kind="ExternalOutput").ap()
    ikind = "ExternalOutput" if debug else "Internal"
    projT_d = nc.dram_tensor("projT_d", [18, 128, S], BF16, kind=ikind).ap()
    accT_d = nc.dram_tensor("accT_d", [8, 128, S], BF16, kind=ikind).ap()
    wup_d = nc.dram_tensor("wup_d", [128, 8, 2 * DFF], BF16, kind="Internal").ap()
    wo_d = nc.dram_tensor("wo_d", [128, 8, D], BF16, kind="Internal").ap()
    wdn_d = nc.dram_tensor("wdn_d", [128, 8, NJ, 128], BF16, kind="Internal").ap()

    xT_v = xT.rearrange("(kc p) t -> p kc t", p=128)
    yT_v = yT.rearrange("(kc p) t -> p kc t", p=128)
    w_in_v = w_in.rearrange("(kc p) c -> p kc c", p=128)
    w_o_v = w_o.rearrange("(kc p) c -> p kc c", p=128)
    w_up_v = w_up.rearrange("(kc p) c -> p kc c", p=128)
    w_down_v = w_down.rearrange("(j p) c -> p j c", p=128)
    projT_v = projT_d.rearrange("c p t -> p c t")
    accT_v = accT_d.rearrange("c p t -> p c t")

    P = Prog(nc)

    def dma(eng, out, in_, lane, reads=(), writes=()):
        return P.op(eng, lambda e, o=out, i=in_: e.dma_start(out=o, in_=i), reads, writes, lane=lane)

    def mm(out, lhsT, rhs, start, stop, reads=(), writes=()):
        return P.op("pe", lambda e, o=out, l=lhsT, r=rhs, s=start, t=stop:
                    e.matmul(o, lhsT=l, rhs=r, start=s, stop=t), reads, writes)

    def act(out, in_, func, reads=(), writes=(), scale=None, bias=None):
        def f(e, o=out, i=in_, fn=func, sc=scale, bi=bias):
            kw = {}
            if sc is not None:
                kw["scale"] = sc
            if bi is not None:
                kw["bias"] = bi
            return e.activation(out=o, in_=i, func=fn, **kw)
        return P.op("act", f, reads, writes)

    def tt_(eng, out, in0, in1, op, reads=(), writes=()):
        return P.op(eng, lambda e, o=out, a=in0, b=in1, p=op: e.tensor_tensor(out=o, in0=a, in1=b, op=p),
                    reads, writes)

    def stt(eng, out, in0, scalar, in1, op0, op1, reads=(), writes=()):
        return P.op(eng, lambda e, o=out, a=in0, s=scalar, b=in1, p0=op0, p1=op1:
                    e.scalar_tensor_tensor(out=o, in0=a, scalar=s, in1=b, op0=p0, op1=p1), reads, writes)

    def ts(eng, out, in0, s1, s2, op0, op1, reads=(), writes=()):
        return P.op(eng, lambda e, o=out, a=in0, x=s1, y=s2, p0=op0, p1=op1:
                    e.tensor_scalar(out=o, in0=a, scalar1=x, scalar2=y, op0=p0, op1=p1), reads, writes)

    def cp(eng, out, in_, reads=(), writes=()):
        if eng == "act":
            return act(out, in_, AF.Copy, reads, writes)
        return P.op(eng, lambda e, o=out, i=in_: e.tensor_copy(out=o, in_=i), reads, writes)

    def mset(eng, ap, val, writes=()):
        return P.op(eng, lambda e, a=ap, v=val: e.memset(a, v), (), writes)

    with ExitStack() as es0:
        def sb(es, name, shape, dt):
            return es.enter_context(nc.sbuf_tensor("sb_" + name, shape, dt))

        pall = es0.enter_context(nc.psum_tensor("pall", [128, 8, 512], F32))
        banks = [pall[:, i, :] for i in range(8)]
        bt = pall[:, 7, :].bitcast(BF16)
        tbk = [Tok() for _ in range(8)]
        tbt = tbk[7]

        cst = sb(es0, "cst", [128, NCST], F32)
        es_sb = sb(es0, "es_sb", [128, 8], F32)
        identf = sb(es0, "identf", [128, 128], F32)
        ident = sb(es0, "ident", [128, 128], BF16)
        ones = sb(es0, "ones", [128, 128], BF16)
        epsl = sb(es0, "epsl", [128, 2], F32)
        t_cst, t_es, t_id, t_ones, t_eps = Tok(), Tok(), Tok(), Tok(), Tok()
        t_wup = [Tok() for _ in range(8)]
        t_wod, t_wdn = Tok(), Tok()

        dma("sp", cst[:], cst_d, "cst", writes=[t_cst])
        act(es_sb[:], cst[:, C_SINK:C_SINK + 8], AF.Exp, reads=[t_cst], writes=[t_es])
        mset("pool", identf[:], 0.0, writes=[t_id])
        P.op("pool", lambda e: e.affine_select(out=identf[:], in_=identf[:], pattern=[[-1, 128]],
                                               compare_op=ALU.not_equal, fill=1.0, base=0,
                                               channel_multiplier=1), reads=[t_id], writes=[t_id])
        cp("dve", ident[:], identf[:], reads=[t_id], writes=[t_id])
        mset("dve", ones[:], 1.0, writes=[t_ones])
        mset("dve", epsl[:, 0:1], LN_EPS, writes=[t_eps])
        mset("dve", epsl[:, 1:2], RMS_EPS, writes=[t_eps])

        t_proj = [Tok() for _ in range(18)]
        with ExitStack() as esA:
            w_in_bf = sb(esA, "w_in_bf", [128, 8, 2304], BF16)
            xb = [sb(esA, f"xb{i}", [128, 8, TT], BF16) for i in range(2)]
            pj = [sb(esA, f"pj{i}", [128, 18, TT], BF16) for i in range(2)]
            t_win = [Tok() for _ in range(8)]
            t_xb = [Tok(), Tok()]
            t_pj = [Tok(), Tok()]
            dma("pool", xb[0][:], xT_v[:, :, 0:TT], "xb0", writes=[t_xb[0]])
            for kc in range(8):
                dma("pool", w_in_bf[:, kc, :], w_in_v[:, kc, :], f"win{kc}", writes=[t_win[kc]])
            dma("pool", xb[1][:], xT_v[:, :, TT:2 * TT], "xb1", writes=[t_xb[1]])
            for tti in range(NTT):
                b = tti % 2
                if 1 <= tti and tti + 1 < NTT:
                    nb_ = (tti + 1) % 2
                    dma("pool", xb[nb_][:], xT_v[:, :, (tti + 1) * TT:(tti + 2) * TT], f"xb{nb_}", writes=[t_xb[nb_]])
                dma("pool", wup_d[:, tti, :], w_up_v[:, tti, :], "wupc", writes=[t_wup[tti]])
                dma("pool", wdn_d[:, tti, :, :], w_down_v[:, :, tti * 128:(tti + 1) * 128], "wdnc", writes=[t_wdn])
                if tti == 0:
                    dma("pool", wo_d, w_o_v, "woc", writes=[t_wod])
                for c0 in range(0, 18, 2):
                    for kc in range(8):
                        for c in (c0, c0 + 1):
                            bk = c % 4
                            mm(banks[bk][:, :], w_in_bf[:, kc, c * 128:(c + 1) * 128], xb[b][:, kc, :],
                               kc == 0, kc == 7, reads=[t_win[kc], t_xb[b]], writes=[tbk[bk]])
                    for c in (c0, c0 + 1):
                        bk = c % 4
                        cp("act" if c % 2 == 0 else "dve", pj[b][:, c, :], banks[bk][:, :],
                           reads=[tbk[bk]], writes=[t_pj[b]])
                dma("sp", projT_v[:, :, tti * TT:(tti + 1) * TT], pj[b][:], f"pj{b}",
                    reads=[t_pj[b]], writes=t_proj)
        P.barrier()

        t_acc_d = [Tok() for _ in range(8)]
        with ExitStack() as esB:
          if upto in ("B", "C"):
            epair = sb(esB, "epair", [128, 32, 512], BF16)
            qT = [sb(esB, f"qT{i}", [128, S], BF16) for i in range(2)]
            kT = [sb(esB, f"kT{i}", [128, S], BF16) for i in range(2)]
            vT = [sb(esB, f"vT{i}", [128, S], BF16) for i in range(2)]
            vaug = [sb(esB, f"vaug{i}", [128, 32, 2, 128], BF16) for i in range(2)]
            acc32 = sb(esB, "acc32", [128, 2, S], F32)
            NPT = 5
            pt = [sb(esB, f"pt{i}", [128, 2, 256], BF16) for i in range(NPT)]
            lnl = [sb(esB, f"lnl{i}", [128, 1024], F32) for i in range(2)]
            rec = [sb(esB, f"rec{i}", [128, 1024], F32) for i in range(2)]
            oT = [sb(esB, f"oT{i}", [128, S], BF16) for i in range(2)]
            t_ep = Tok()
            t_q = [Tok(), Tok()]
            t_k = [Tok(), Tok()]
            t_v = [Tok(), Tok()]
            t_va = [[Tok() for _ in range(4)] for _ in range(2)]
            t_acc = [Tok() for _ in range(32)]
            t_pt = [Tok() for _ in range(NPT)]
            t_lnl = [Tok(), Tok()]
            t_rec = [Tok(), Tok()]
            t_oT = [Tok(), Tok()]

            for h4 in range(8):
                dma("pool", epair[:, 4 * h4:4 * h4 + 4, :], etab_d[:, 4 * h4:4 * h4 + 4, :], "etab", writes=[t_ep])
            for i in range(2):
                mset("pool", vaug[i][:, :, :, 64:128], 1.0, writes=t_va[i])

            def pair_chunks(p):
                if p < 4:
                    return p, 4, 5
                j = p - 4
                return 6 + j, 10 + j, 14 + j

            def issue_loads(p):
                cq, ck, cv = pair_chunks(p)
                qi = p % 2
                dma("sp", qT[qi][:], projT_d[cq], f"q{qi}", reads=[t_proj[cq]], writes=[t_q[qi]])
                if p == PAIRS[0] or p >= 4:
                    ki = 0 if p < 4 else (p + 1) % 2
                    dma("sp", kT[ki][:], projT_d[ck], f"k{ki}", reads=[t_proj[ck]], writes=[t_k[ki]])
                    dma("sp", vT[ki][:], projT_d[cv], f"v{ki}", reads=[t_proj[cv]], writes=[t_v[ki]])

            issue_loads(PAIRS[0])
            npat = 0
            bseq = 0
            oseq = 0
            for pidx, p in enumerate(PAIRS):
                if pidx + 1 < len(PAIRS):
                    issue_loads(PAIRS[pidx + 1])
                qi = p % 2
                ki = 0 if p < 4 else (p + 1) % 2
                pats = [(1, 0)] if p < 4 else [(1, 0), (4, 1), (16, 2)]
                for (r, pi) in pats:
                    ti = p if p < 4 else 4 + (p - 4) * 3 + pi
                    nb = 32 // r
                    reuse_v = (0 < p < 4) and PAIRS[0] == 0
                    if reuse_v:
                        va, tva = vaug[0], t_va[0]
                    else:
                        va = vaug[npat % 2]
                        tva = t_va[npat % 2]
                        npat += 1
                    blocks = [(rho, n) for rho in range(r) for n in range(nb)]

                    def tstart(rho, n):
                        return rho + r * 128 * n

                    for g in range(4 if (DBG_LEVEL >= 1 and not reuse_v) else 0):
                        for i in range(8):
                            rho, n = blocks[8 * g + i]
                            st = tstart(rho, n)
                            P.op("pe", lambda e, o=bt[:, i * 128:(i + 1) * 128],
                                 a=vT[ki][:, st:st + r * 127 + 1:r]: e.transpose(o, a, ident[:]),
                                 reads=[t_v[ki], t_id], writes=[tbt])
                        cp("dve", va[:, 8 * g:8 * g + 8, :, 0:64],
                           bt[:, :].rearrange("p (b h d) -> p b h d", b=8, h=2),
                           reads=[tbt], writes=[tva[g]])

                    info = {}

                    def emit_scores(bi):
                        nonlocal bseq
                        rho, n = blocks[bi]
                        st = tstart(rho, n)
                        last = (n == nb - 1)
                        nq = 128 if last else 256
                        sbk = 2 * (bseq % 2)
                        pti = bseq % NPT
                        bseq += 1
                        info[bi] = pti
                        use_bias = False
                        if use_bias:
                            for h in range(2):
                                mm(banks[sbk + h][:, 0:nq], ident[:], epair[:, 16 + ti, h * 256:h * 256 + nq],
                                   True, False, reads=[t_id, t_ep], writes=[tbk[sbk + h]])
                        for h in range(2):
                            mm(banks[sbk + h][:, 0:nq],
                               kT[ki][h * 64:(h + 1) * 64, st:st + r * 127 + 1:r],
                               qT[qi][h * 64:(h + 1) * 64, st:st + r * (nq - 1) + 1:r],
                               not use_bias, True, reads=[t_k[ki], t_q[qi]], writes=[tbk[sbk + h]])
                        sv = pall[:, sbk:sbk + 2, 0:nq]
                        act(pt[pti][:, :, 0:nq], sv, AF.Exp, reads=[tbk[sbk], tbk[sbk + 1]], writes=[t_pt[pti]],
                            scale=0.125)
                        if not use_bias:
                            ev = epair[:, ti, :].rearrange("p (h q) -> p h q", h=2)[:, :, 0:nq]
                            tt_("dve", pt[pti][:, :, 0:nq], pt[pti][:, :, 0:nq], ev, ALU.mult,
                                reads=[t_pt[pti], t_ep], writes=[t_pt[pti]])

                    def emit_pv(bi, first):
                        nonlocal oseq
                        rho, n = blocks[bi]
                        obk = 4 + ((oseq // 2) % 3)
                        oseq += 1
                        pc = info[bi]
                        blk = bi
                        for h in range(2):
                            c0 = h * 256 + (n % 2) * 128
                            o_ap = banks[obk][:, c0:c0 + 128]
                            if n > 0:
                                pp = info[bi - 1]
                                mm(o_ap, va[:, blk - 1, h, :], pt[pp][:, h, 128:256], True, False,
                                   reads=[tva[(blk - 1) // 8], t_pt[pp]], writes=[tbk[obk]])
                                mm(o_ap, va[:, blk, h, :], pt[pc][:, h, 0:128], False, True,
                                   reads=[tva[blk // 8], t_pt[pc]], writes=[tbk[obk]])
                            else:
                                mm(o_ap, va[:, blk, h, :], pt[pc][:, h, 0:128], True, True,
                                   reads=[tva[blk // 8], t_pt[pc]], writes=[tbk[obk]])
                        if n % 2 == 0:
                            return
                        st0 = tstart(rho, n - 1)
                        ov = banks[obk][:, :].rearrange("p (h q) -> p h q", h=2)
                        av = acc32[:, :, st0:st0 + r * 255 + 1:r]
                        nb0 = st0 // 128
                        nb1 = (st0 + r * 255) // 128
                        toks = t_acc[nb0:nb1 + 1]
                        if first:
                            cp("dve", av, ov, reads=[tbk[obk]], writes=toks)
                        else:
                            tt_("dve", av, ov, av, ALU.add, reads=[tbk[obk]] + toks, writes=toks)

                    LOOK = 2
                    for i in range(len(blocks) + LOOK):
                        if i < len(blocks) and DBG_LEVEL >= 2:
                            emit_scores(i)
                        if i - LOOK >= 0 and DBG_LEVEL >= 3:
                            emit_pv(i - LOOK, pi == 0)

                oi = p % 2
                for h in range(0 if not SKIP_NORM else 2, 2):
                    head = (p + 4 * h) if p < 4 else 0
                    for c4 in range(4):
                        sl = slice(c4 * 1024, (c4 + 1) * 1024)
                        toks = t_acc[c4 * 8:(c4 + 1) * 8]
                        li = (h * 4 + c4) % 2
                        bias = es_sb[64:128, head:head + 1] if p < 4 else None
                        act(lnl[li][0:64, :], acc32[64:128, h, sl], AF.Ln,
                            reads=toks + [t_es], writes=[t_lnl[li]], bias=bias)
                        act(rec[li][0:64, :], lnl[li][0:64, :], AF.Exp, reads=[t_lnl[li]],
                            writes=[t_rec[li]], scale=-1.0)
                        tt_("dve", oT[oi][h * 64:(h + 1) * 64, sl], acc32[0:64, h, sl], rec[li][0:64, :], ALU.mult,
                            reads=toks + [t_rec[li]], writes=[t_oT[oi]])
                dma("sp", accT_d[p], oT[oi][:], f"oT{oi}", reads=[t_oT[oi]], writes=[t_acc_d[p]])
        P.barrier()

        with ExitStack() as esC:
          if upto == "C":
            wo_bf = sb(esC, "wo_bf", [128, 8, D], BF16)
            NWD = 2
            wd = [sb(esC, f"wd{i}", [128, NJ, 128], BF16) for i in range(NWD)]
            wu = [sb(esC, f"wu{i}", [128, 8, 512], BF16) for i in range(2)]
            at = sb(esC, "at", [128, 8, TT], BF16)
            sq = sb(esC, "sq", [128, 8, TT], BF16)
            x32 = sb(esC, "x32", [128, 8, TT], F32)
            h1 = [sb(esC, f"h1_{i}", [128, 8, TT], F32) for i in range(3)]
            h1b = [sb(esC, f"h1b_{i}", [128, 8, TT], BF16) for i in range(2)]
            gTs = [sb(esC, f"gT{i}", [128, NJ, TT], BF16) for i in range(2)]
            ygs = [sb(esC, f"yg{i}", [128, TT], F32) for i in range(2)]
            yvs = [sb(esC, f"yv{i}", [128, TT], F32) for i in range(2)]
            tmpA = [sb(esC, f"tmpA{i}", [128, TT], F32) for i in range(2)]
            stA = [sb(esC, f"stA{i}", [128, TT], F32) for i in range(4)]
            uhs = [sb(esC, f"uh{i}", [128, 2 * NJ, 2], F32) for i in range(2)]
            t_wo = Tok()
            t_wd = [Tok() for _ in range(NWD)]
            t_wu = [Tok(), Tok()]
            t_at = [Tok() for _ in range(8)]
            t_sq = [Tok() for _ in range(8)]
            t_x32 = [Tok() for _ in range(8)]
            t_h1 = [[Tok() for _ in range(8)] for _ in range(3)]
            t_h1b = [[Tok() for _ in range(8)] for _ in range(2)]
            t_gTs = [[Tok() for _ in range(NJ)] for _ in range(2)]
            t_ygs, t_yvs = [Tok(), Tok()], [Tok(), Tok()]
            t_tmpA = [Tok(), Tok()]
            t_stA = [Tok() for _ in range(4)]
            t_uhs = [[Tok() for _ in range(2 * NJ)] for _ in range(2)]

            dma("sp", wo_bf[:], wo_d, "wo", reads=[t_wod], writes=[t_wo])

            def col(c):
                return cst[:, c:c + 1]

            def ln_units(src, t_src, gcol, bcol, dst32, t_d32, dstbf, t_dbf, cpb, t_cpb, sqb, t_sqb,
                         st, t_st, tmp, t_tmp, bkA, bkB):
                mean, mv, lr, nmr = st
                units = []

                def u_sq(m0):
                    def f():
                        for m in range(m0, m0 + 2):
                            cp("dve", cpb(m), src[:, m, :], reads=[t_src[m]], writes=[t_cpb[m]])
                            act(sqb(m), src[:, m, :], AF.Square, reads=[t_src[m]], writes=[t_sqb[m]])
                    return f
                units += [u_sq(0), u_sq(2), u_sq(4), u_sq(6)]

                def u_mm():
                    for m in range(8):
                        mm(banks[bkA][:, :], ones[:], cpb(m), m == 0, m == 7, reads=[t_ones, t_cpb[m]], writes=[tbk[bkA]])
                    for m in range(8):
                        mm(banks[bkB][:, :], ones[:], sqb(m), m == 0, m == 7, reads=[t_ones, t_sqb[m]], writes=[tbk[bkB]])
                units.append(u_mm)

                def u_stats():
                    act(mean[:], banks[bkA][:, :], AF.Copy, reads=[tbk[bkA]], writes=[t_st[0]], scale=1.0 / D)
                    act(mv[:], banks[bkA][:, :], AF.Square, reads=[tbk[bkA]], writes=[t_st[1]], scale=1.0 / D)
                    stt("dve", mv[:], banks[bkB][:, :], 1.0 / D, mv[:], ALU.mult, ALU.subtract,
                        reads=[tbk[bkB], t_st[1]], writes=[t_st[1]])
                    act(lr[:], mv[:], AF.Ln, reads=[t_st[1], t_eps], writes=[t_st[2]], bias=epsl[:, 0:1])
                    act(lr[:], lr[:], AF.Exp, reads=[t_st[2]], writes=[t_st[2]], scale=-0.5)
                    stt("dve", nmr[:], mean[:], -1.0, lr[:], ALU.mult, ALU.mult,
                        reads=[t_st[0], t_st[2]], writes=[t_st[3]])
                units.append(u_stats)

                def u_norm(m0):
                    def f():
                        for m in range(m0, m0 + 2):
                            tb = m % 2
                            tt_("dve", tmp[tb][:], src[:, m, :], lr[:], ALU.mult, reads=[t_src[m], t_st[2]], writes=[t_tmp[tb]])
                        for m in range(m0, m0 + 2):
                            tb = m % 2
                            tt_("dve", tmp[tb][:], tmp[tb][:], nmr[:], ALU.add, reads=[t_tmp[tb], t_st[3]], writes=[t_tmp[tb]])
                        for m in range(m0, m0 + 2):
                            tb = m % 2
                            ts("pool", dst32[:, m, :], tmp[tb][:], col(gcol + m), col(bcol + m), ALU.mult, ALU.add,
                               reads=[t_tmp[tb], t_cst], writes=[t_d32[m]])
                            if dstbf is not None:
                                ts("pool", dstbf[:, m, :], tmp[tb][:], col(gcol + m), col(bcol + m), ALU.mult, ALU.add,
                                   reads=[t_tmp[tb], t_cst], writes=[t_dbf[m]])
                    return f
                units += [u_norm(0), u_norm(2), u_norm(4), u_norm(6)]
                return units

            wu_list = [(t, jj) for t in range(NTT) for jj in range(NJ // 2)]
            wu_issued = [0]

            def wu_issue_upto(k):
                while wu_issued[0] <= k and wu_issued[0] < len(wu_list):
                    i = wu_issued[0]
                    b = i % 2
                    jj = wu_list[i][1]
                    dma("sp", wu[b][:], wup_d[:, :, jj * 512:(jj + 1) * 512], f"wu{b}", reads=t_wup, writes=[t_wu[b]])
                    wu_issued[0] += 1

            wd_issued = [0]

            def wd_issue_upto(k):
                while wd_issued[0] <= k and wd_issued[0] < NTT * 8:
                    i = wd_issued[0]
                    b = i % NWD
                    m = i % 8
                    dma("sp", wd[b][:], wdn_d[:, m, :, :], f"wdl{b}", reads=[t_wdn], writes=[t_wd[b]])
                    wd_issued[0] += 1

            def F_units(t):
                hb = t % 2
                h3 = t % 3
                tsl = slice(t * TT, (t + 1) * TT)
                units = []

                def u_load():
                    for m in range(8):
                        dma("sp", at[:, m, :], accT_v[:, m, tsl], f"at{m}", reads=[t_acc_d[m]], writes=[t_at[m]])
                    for m in range(8):
                        dma("sp", x32[:, m, :], xT_v[:, m, tsl], f"x32_{m}", writes=[t_x32[m]])
                units.append(u_load)

                def u_rsq(c0):
                    def f():
                        for c in range(c0, c0 + 4):
                            tt_("dve", sq[:, c, :], at[:, c, :], at[:, c, :], ALU.mult, reads=[t_at[c]], writes=[t_sq[c]])
                    return f
                units += [u_rsq(0), u_rsq(4)]

                def u_rmm():
                    for gi in range(2):
                        for i, c in enumerate(range(4 * gi, 4 * gi + 4)):
                            mm(banks[gi][:, :], ones[:], sq[:, c, :], i == 0, i == 3, reads=[t_ones, t_sq[c]], writes=[tbk[gi]])
                        a = stA[2 * gi]
                        act(a[:], banks[gi][:, :], AF.Ln, reads=[tbk[gi], t_eps], writes=[t_stA[2 * gi]],
                            scale=1.0 / 512, bias=epsl[:, 1:2])
                        act(a[:], a[:], AF.Exp, reads=[t_stA[2 * gi]], writes=[t_stA[2 * gi]], scale=-0.5)
                units.append(u_rmm)

                def u_rn():
                    for gi in range(2):
                        for c in range(4 * gi, 4 * gi + 4):
                            stt("dve", at[:, c, :], at[:, c, :], col(C_GATT + c), stA[2 * gi][:], ALU.mult, ALU.mult,
                                reads=[t_at[c], t_stA[2 * gi], t_cst], writes=[t_at[c]])
                units.append(u_rn)

                def u_wo(m):
                    def f():
                        bk = m % 2
                        for kc in range(8):
                            mm(banks[bk][:, :], wo_bf[:, kc, m * 128:(m + 1) * 128], at[:, kc, :], kc == 0, kc == 7,
                               reads=[t_wo, t_at[kc]], writes=[tbk[bk]])
                        stt("dve", x32[:, m, :], x32[:, m, :], ALPHA, banks[bk][:, :], ALU.mult, ALU.add,
                            reads=[t_x32[m], tbk[bk]], writes=[t_x32[m]])
                    return f
                units += [u_wo(m) for m in range(8)]
                units += ln_units(x32, t_x32, C_L1G, C_L1B, h1[h3], t_h1[h3], h1b[hb], t_h1b[hb],
                                  lambda m: at[:, m, :], t_at, lambda m: sq[:, m, :], t_sq,
                                  stA, t_stA, tmpA, t_tmpA, 0, 1)
                return units

            UPB = [2, 3, 4, 5, 7]

            def up_units(t):
                hb = t % 2
                gT, t_gT = gTs[t % 2], t_gTs[t % 2]
                units = []

                def u_up(j):
                    def f():
                        k = t * (NJ // 2) + j // 2
                        if j % 2 == 0:
                            wu_issue_upto(k + 1)
                        cur = k % 2
                        woff = (j % 2) * 256
                        yg, yv, t_yg, t_yv = ygs[j % 2], yvs[j % 2], t_ygs[j % 2], t_yvs[j % 2]
                        hv = []
                        for kc in range(8):
                            for half in range(2):
                                bk = UPB[(2 * j + half) % 5]
                                mm(banks[bk][:, :], wu[cur][:, kc, woff + half * 128:woff + half * 128 + 128],
                                   h1b[hb][:, kc, :], kc == 0, kc == 7, reads=[t_wu[cur], t_h1b[hb][kc]], writes=[tbk[bk]])
                        for half, (ybuf, t_y) in enumerate(((yg, t_yg), (yv, t_yv))):
                            bk = UPB[(2 * j + half) % 5]
                            cc = j + half * NJ
                            w0, w1, w2 = (col(C_CW + cc * 3 + kk) for kk in range(3))
                            hv.append((ybuf, t_y, bk, cc, w0, w1, w2))
                        for (ybuf, t_y, bk, cc, w0, w1, w2) in hv:
                            act(ybuf[:], banks[bk][:, :], AF.Identity, reads=[tbk[bk], t_cst], writes=[t_y],
                                scale=w2, bias=col(C_CB + cc))
                        for (ybuf, t_y, bk, cc, w0, w1, w2) in hv:
                            stt("dve", ybuf[:, 1:TT], banks[bk][:, 0:TT - 1], w1, ybuf[:, 1:TT], ALU.mult, ALU.add,
                                reads=[tbk[bk], t_y], writes=[t_y])
                        for (ybuf, t_y, bk, cc, w0, w1, w2) in hv:
                            stt("dve", ybuf[:, 2:TT], banks[bk][:, 0:TT - 2], w0, ybuf[:, 2:TT], ALU.mult, ALU.add,
                                reads=[tbk[bk], t_y], writes=[t_y])
                        uh, t_uh = uhs[t % 2], t_uhs[t % 2]
                        uhn, t_uhn = uhs[(t + 1) % 2], t_uhs[(t + 1) % 2]
                        if t + 1 < NTT:
                            for (ybuf, t_y, bk, cc, w0, w1, w2) in hv:
                                cp("dve", uhn[:, cc, :], banks[bk][:, TT - 2:TT], reads=[tbk[bk]], writes=[t_uhn[cc]])
                        if t > 0:
                            for (ybuf, t_y, bk, cc, w0, w1, w2) in hv:
                                stt("dve", ybuf[:, 0:1], uh[:, cc, 1:2], w1, ybuf[:, 0:1], ALU.mult, ALU.add,
                                    reads=[t_uh[cc], t_y], writes=[t_y])
                            for (ybuf, t_y, bk, cc, w0, w1, w2) in hv:
                                stt("dve", ybuf[:, 0:2], uh[:, cc, 0:2], w0, ybuf[:, 0:2], ALU.mult, ALU.add,
                                    reads=[t_uh[cc], t_y], writes=[t_y])
                        act(yg[:], yg[:], AF.Gelu_apprx_tanh, reads=[t_yg], writes=[t_yg])
                        tt_("pool", gT[:, j, :], yg[:], yv[:], ALU.mult, reads=[t_yg, t_yv], writes=[t_gT[j]])
                    return f
                units += [u_up(j) for j in range(NJ)]
                return units

            def down_units(t):
                h3 = t % 3
                gT, t_gT = gTs[t % 2], t_gTs[t % 2]
                units = []

                def u_down(m):
                    def f():
                        k = t * 8 + m
                        wd_issue_upto(k + 1)
                        b = k % NWD
                        bk = 6
                        for j in range(NJ):
                            mm(banks[bk][:, :], wd[b][:, j, :], gT[:, j, :], j == 0, j == NJ - 1,
                               reads=[t_wd[b], t_gT[j]], writes=[tbk[bk]])
                        stt("dve", h1[h3][:, m, :], h1[h3][:, m, :], ALPHA, banks[bk][:, :], ALU.mult, ALU.add,
                            reads=[t_h1[h3][m], tbk[bk]], writes=[t_h1[h3][m]])
                    return f
                units += [u_down(m) for m in range(8)]
                return units

            def L2_units(t):
                hb = t % 3
                tsl = slice(t * TT, (t + 1) * TT)
                units = ln_units(h1[hb], t_h1[hb], C_L2G, C_L2B, h1[hb], t_h1[hb], None, None,
                                 lambda m: at[:, m, :], t_at, lambda m: sq[:, m, :], t_sq,
                                 stA, t_stA, tmpA, t_tmpA, 0, 1)

                def u_store():
                    for m in range(8):
                        dma("sp", yT_v[:, m, tsl], h1[hb][:, m, :], f"out{m}", reads=[t_h1[hb][m]])
                units.append(u_store)
                return units

            for u in F_units(0):
                u()
            wd_issue_upto(0)
            DPOS = [2, 5, 9, 13, 16, 20, 24, 27]
            A = []
            base = {}
            for t in range(NTT + 1):
                base[t] = len(A)
                ups = up_units(t) if t < NTT else []
                downs = down_units(t - 1) if t >= 1 else []
                if ups and downs:
                    ui = di = 0
                    for k in range(30):
                        if di < 8 and k == DPOS[di]:
                            A.append(downs[di]); di += 1
                        else:
                            A.append(ups[ui]); ui += 1
                else:
                    A += ups + downs
            Bs = []
            pos_floor = 0
            for t in range(NTT + 1):
                FOFF = [0, 0, 1, 3, 4, 6, 6, 7, 7, 8, 8, 9, 9, 10, 11, 12, 13, 15, 16, 17, 17, 18, 18]
                LOFF = [0, 1, 2, 3, 5, 6, 7, 7, 8, 8, 9]
                if t + 1 < NTT:
                    fu = F_units(t + 1)
                    assert len(fu) == len(FOFF)
                    p0 = max(base[t], pos_floor)
                    for i, u in enumerate(fu):
                        Bs.append((p0 + FOFF[i], u))
                    pos_floor = p0 + FOFF[-1] + 1
                if t >= 1:
                    lu = L2_units(t - 1)
                    assert len(lu) == len(LOFF)
                    p0 = max(base[t] + (27 if t < NTT else 7), pos_floor)
                    for i, u in enumerate(lu):
                        Bs.append((p0 + LOFF[i], u))
                    pos_floor = p0 + LOFF[-1] + 1
            bi = 0
            for ai, a in enumerate(A):
                a()
                while bi < len(Bs) and Bs[bi][0] <= ai:
                    Bs[bi][1]()
                    bi += 1
            while bi < len(Bs):
                Bs[bi][1]()
                bi += 1
          P.op("sp", None, deps=list(P.lane_last.values()))
          P.op("pool", None, deps=list(P.lane_last.values()))
          P.emit()
    return nc


def _perm_a():
    idx = []
    for c in range(4):
        idx += list(range(c * 64, c * 64 + 64)) + list(range((c + 4) * 64, (c + 4) * 64 + 64))
    return np.array(idx)


def _etab():
    k = np.arange(128)[:, None].astype(np.float64)
    q = np.arange(256)[None, :].astype(np.float64)
    dist = q - k
    et = np.zeros((128, 32, 512), np.float32)

    def tab(slope, md):
        m = (dist >= 0) & (dist <= md)
        return (np.where(m, np.exp(-slope * np.maximum(dist, 0)), 0.0),
                np.where(m, -8.0 * slope * np.maximum(dist, 0), -30000.0))
    for p in range(4):
        for h, head in enumerate((p, p + 4)):
            e, bsl = tab(2.0 ** (-(head + 1)), 127)
            et[:, p, h * 256:(h + 1) * 256] = e
            et[:, 16 + p, h * 256:(h + 1) * 256] = bsl
    for j in range(4):
        for pi, r in enumerate((1, 4, 16)):
            for h, head in enumerate((2 * j, 2 * j + 1)):
                e, bsl = tab(2.0 ** (-(head + 1)) * r, 128)
                et[:, 4 + j * 3 + pi, h * 256:(h + 1) * 256] = e
                et[:, 16 + 4 + j * 3 + pi, h * 256:(h + 1) * 256] = bsl
    return et


_NC_CACHE = {}


def kernel(x, w_in, norm_a_g, norm_b_g, sinks_a, w_o, ln1_g, ln1_b, w_up, conv_w, conv_b, w_down, ln2_g, ln2_b):
    x = np.asarray(x, np.float32)
    w_in = np.asarray(w_in, np.float32)
    w_o = np.asarray(w_o, np.float32)
    w_up = np.asarray(w_up, np.float32)
    w_down = np.asarray(w_down, np.float32)
    pa = _perm_a()
    cols = np.concatenate([pa, np.arange(512, 2304)])
    w_in_p = np.ascontiguousarray(w_in[:, cols])
    rows = np.concatenate([pa, np.arange(512, 1024)])
    w_o_p = np.ascontiguousarray(w_o[rows, :])
    ucols = np.concatenate([np.concatenate([np.arange(j * 128, (j + 1) * 128),
                                            np.arange(DFF + j * 128, DFF + (j + 1) * 128)]) for j in range(NJ)])
    w_up_p = np.ascontiguousarray(w_up[:, ucols])
    cst = np.zeros((128, NCST), np.float32)
    g_att = np.concatenate([np.asarray(norm_a_g, np.float32)[pa], np.asarray(norm_b_g, np.float32)])
    cst[:, C_GATT:C_GATT + 8] = g_att.reshape(8, 128).T
    cst[:, C_L1G:C_L1G + 8] = np.asarray(ln1_g, np.float32).reshape(8, 128).T
    cst[:, C_L1B:C_L1B + 8] = np.asarray(ln1_b, np.float32).reshape(8, 128).T
    cst[:, C_L2G:C_L2G + 8] = np.asarray(ln2_g, np.float32).reshape(8, 128).T
    cst[:, C_L2B:C_L2B + 8] = np.asarray(ln2_b, np.float32).reshape(8, 128).T
    cst[:, C_CB:C_CB + 2 * NJ] = np.asarray(conv_b, np.float32).reshape(2 * NJ, 128).T
    cw = np.asarray(conv_w, np.float32).reshape(3, 2 * NJ, 128)
    cst[:, C_CW:C_CW + 6 * NJ] = cw.transpose(2, 1, 0).reshape(128, 6 * NJ)
    cst[:, C_SINK:C_SINK + 8] = np.asarray(sinks_a, np.float32)[None, :]
    etab = _etab()

    nc = build_nc()
    in_maps = []
    for b in range(8):
        in_maps.append({"xT": np.ascontiguousarray(x[b].T), "w_in": w_in_p, "w_o": w_o_p, "w_up": w_up_p,
                        "w_down": w_down, "cst": cst, "etab": etab})
    res = run_bass_kernel_spmd(nc, in_maps, core_ids=list(range(8)))
    out = np.stack([np.ascontiguousarray(res.results[b]["yT"].T) for b in range(8)], axis=0)
    return out.astype(np.float32)
```
